# Optimizing a Trainium2 kernel written in Bass

```python
import jax, jax.numpy as jnp
from jax import lax
import numpy as np

D_MODEL = 2048
BATCH = 16
SEQ = 2048
DEPTH = 2

GRID_W = 64
CTX_LEN = 256
D_CONF = D_MODEL
D_SCONV = D_MODEL
D_LRU = D_MODEL
LRU_HEADS = 8
LRU_HEAD_DIM = D_LRU // LRU_HEADS
D_FF = 3 * D_MODEL
CONF_WIDTH = 31
CONF_PAD = (CONF_WIDTH - 1) // 2
SCONV_WIDTH = 3
LRU_CONV_WIDTH = 4
LRU_PAD_L = 2
LRU_PAD_R = 1
FFN_CONV_WIDTH = 3
RG_C = 8.0
EPS = 1e-6
IN_SIZES = (D_CONF, D_CONF, D_SCONV, D_SCONV, D_SCONV, D_LRU, D_LRU, D_MODEL, D_MODEL, D_MODEL)
N_IN = sum(IN_SIZES)
IN_SPLITS = tuple(sum(IN_SIZES[:i + 1]) for i in range(len(IN_SIZES) - 1))
IDX_LRU_X = 5
LRU_X_OFF = sum(IN_SIZES[:IDX_LRU_X])

kernel_name = "hybrid_conformer_shortconv_rglru_dit"


def rms_norm(t, g):
    tf = t.astype(jnp.float32)
    y = tf * lax.rsqrt(jnp.mean(tf * tf, axis=-1, keepdims=True) + EPS)
    return (y * g.astype(jnp.float32)).astype(t.dtype)


def layer_norm(t, g, b):
    tf = t.astype(jnp.float32)
    mu = jnp.mean(tf, axis=-1, keepdims=True)
    var = jnp.mean(jnp.square(tf - mu), axis=-1, keepdims=True)
    y = (tf - mu) * lax.rsqrt(var + EPS)
    return (y * g.astype(jnp.float32) + b.astype(jnp.float32)).astype(t.dtype)


def modulate(t, shift, scale):
    return t * (1 + scale) + shift


def dwconv_line(t, w, pad_l, pad_r):
    return lax.conv_general_dilated(
        t, w[:, None, :].astype(t.dtype), window_strides=(1,), padding=((pad_l, pad_r),),
        dimension_numbers=("NWC", "WIO", "NWC"), feature_group_count=t.shape[-1])


def dwconv_rows(t, w, pad_l, pad_r, rows):
    n, l, ch = t.shape
    y = dwconv_line(t.reshape(n * rows, GRID_W, ch), w, pad_l, pad_r)
    return y.reshape(n, l, ch)


def dwconv_cols(t, w, pad_l, pad_r, rows):
    n, l, ch = t.shape
    t4 = t.reshape(n, rows, GRID_W, ch)
    y = lax.conv_general_dilated(
        t4, w[:, None, None, :].astype(t.dtype), window_strides=(1, 1),
        padding=((pad_l, pad_r), (0, 0)), dimension_numbers=("NHWC", "HWIO", "NHWC"),
        feature_group_count=ch)
    return y.reshape(n, l, ch)


def _scan_combine(left, right):
    a_l, b_l = left
    a_r, b_r = right
    return a_l * a_r, a_r * b_l + b_r


def rglru_coeffs(u, w_a, b_a, w_x, b_x, lam):
    n, l, _ = u.shape
    uh = u.reshape(n, l, LRU_HEADS, LRU_HEAD_DIM)
    pre_r = jnp.einsum("nlhi,hij->nlhj", uh, w_a).reshape(n, l, D_LRU) + b_a
    pre_i = jnp.einsum("nlhi,hij->nlhj", uh, w_x).reshape(n, l, D_LRU) + b_x
    r = jax.nn.sigmoid(pre_r.astype(jnp.float32))
    i = jax.nn.sigmoid(pre_i.astype(jnp.float32))
    log_a = -RG_C * r * jax.nn.softplus(-lam.astype(jnp.float32))
    a = jnp.exp(log_a)
    b = jnp.sqrt(-jnp.expm1(2.0 * log_a)) * (i * u.astype(jnp.float32))
    return a, b


def rglru_bidir(v_lat, v_ctx, w_dw, b_dw, w_a, b_a, w_x, b_x, lam, need_ctx):
    u_lat = dwconv_line(v_lat, w_dw, LRU_PAD_L, LRU_PAD_R) + b_dw
    u_ctx = dwconv_line(v_ctx, w_dw, LRU_PAD_L, LRU_PAD_R) + b_dw
    h_lat_dirs = []
    h_ctx_dirs = []
    for d, rev in enumerate((False, True)):
        a_c, b_c = rglru_coeffs(u_ctx, w_a[d], b_a[d], w_x[d], b_x[d], lam[d])
        _, hc = lax.associative_scan(_scan_combine, (a_c, b_c), reverse=rev, axis=1)
        h0 = hc[:, 0] if rev else hc[:, -1]
        a_l, b_l = rglru_coeffs(u_lat, w_a[d], b_a[d], w_x[d], b_x[d], lam[d])
        a_cum, b_cum = lax.associative_scan(_scan_combine, (a_l, b_l), reverse=rev, axis=1)
        h_lat_dirs.append(a_cum * h0[:, None, :] + b_cum)
        h_ctx_dirs.append(hc)
    h_lat = h_lat_dirs[0] + h_lat_dirs[1]
    h_ctx = (h_ctx_dirs[0] + h_ctx_dirs[1]) if need_ctx else None
    return h_lat, h_ctx


def mixer_merge(parts, h_rec, conv, w_dw_a, ln_g, ln_b, w_out_a, w_dw_b, w_out_b, w_out_c, w_o):
    a_val, a_gate, s_b, s_c, s_h, _, r_y, g_a, g_b, g_c = parts
    ya = a_val * jax.nn.sigmoid(a_gate)
    ya = conv(ya, w_dw_a, CONF_PAD, CONF_PAD)
    ya = jax.nn.silu(layer_norm(ya, ln_g, ln_b)) @ w_out_a
    yb = (s_b * conv(s_c * s_h, w_dw_b, 1, 1)) @ w_out_b
    yc = (h_rec.astype(r_y.dtype) * jax.nn.gelu(r_y)) @ w_out_c
    merged = jax.nn.sigmoid(g_a) * ya + jax.nn.sigmoid(g_b) * yb + jax.nn.sigmoid(g_c) * yc
    return merged @ w_o


def conv_ffn(t, w_up, w_dw, w_down, conv):
    gate, val = jnp.split(t @ w_up, 2, axis=-1)
    gate = conv(gate, w_dw, 1, 1)
    return (jax.nn.gelu(gate) * val) @ w_down


def setup_inputs(seed: int = 0) -> dict:
    key = jax.random.key(seed)
    ks = iter(jax.random.split(key, 40))

    def nrm(shape, scale):
        return jax.random.normal(next(ks), shape, jnp.float32) * scale

    u = jax.random.uniform(next(ks), (DEPTH, 2, D_LRU), jnp.float32, minval=0.9, maxval=0.999)
    s = u ** (1.0 / RG_C)
    lam = jnp.log(s) - jnp.log1p(-s)
    return {
        "x": nrm((BATCH, SEQ, D_MODEL), 1.0),
        "c": nrm((BATCH, D_MODEL), 1.0),
        "ctx": nrm((BATCH, CTX_LEN, D_MODEL), 1.0),
        "c_ctx": nrm((D_MODEL,), 1.0),
        "w_mod": nrm((DEPTH, D_MODEL, 6 * D_MODEL), 0.5 * D_MODEL ** -0.5),
        "b_mod": nrm((DEPTH, 6 * D_MODEL), 0.02),
        "g_norm1": 1.0 + nrm((DEPTH, D_MODEL), 0.02),
        "w_in": nrm((DEPTH, D_MODEL, N_IN), D_MODEL ** -0.5),
        "b_in": nrm((DEPTH, N_IN), 0.02),
        "w_dw_a": nrm((DEPTH, CONF_WIDTH, D_CONF), CONF_WIDTH ** -0.5),
        "ln_a_g": 1.0 + nrm((DEPTH, D_CONF), 0.02),
        "ln_a_b": nrm((DEPTH, D_CONF), 0.02),
        "w_out_a": nrm((DEPTH, D_CONF, D_MODEL), D_CONF ** -0.5),
        "w_dw_b": nrm((DEPTH, SCONV_WIDTH, D_SCONV), SCONV_WIDTH ** -0.5),
        "w_out_b": nrm((DEPTH, D_SCONV, D_MODEL), D_SCONV ** -0.5),
        "w_dw_c": nrm((DEPTH, LRU_CONV_WIDTH, D_LRU), LRU_CONV_WIDTH ** -0.5),
        "b_dw_c": nrm((DEPTH, D_LRU), 0.02),
        "w_rg_a": nrm((DEPTH, 2, LRU_HEADS, LRU_HEAD_DIM, LRU_HEAD_DIM), LRU_HEAD_DIM ** -0.5),
        "b_rg_a": nrm((DEPTH, 2, D_LRU), 0.02),
        "w_rg_x": nrm((DEPTH, 2, LRU_HEADS, LRU_HEAD_DIM, LRU_HEAD_DIM), LRU_HEAD_DIM ** -0.5),
        "b_rg_x": nrm((DEPTH, 2, D_LRU), 0.02),
        "lam": lam,
        "w_out_c": nrm((DEPTH, D_LRU, D_MODEL), D_LRU ** -0.5),
        "w_o": nrm((DEPTH, D_MODEL, D_MODEL), D_MODEL ** -0.5),
        "g_norm2": 1.0 + nrm((DEPTH, D_MODEL), 0.02),
        "w_up": nrm((DEPTH, D_MODEL, 2 * D_FF), D_MODEL ** -0.5),
        "w_dw_f": nrm((DEPTH, FFN_CONV_WIDTH, D_FF), FFN_CONV_WIDTH ** -0.5),
        "w_down": nrm((DEPTH, D_FF, D_MODEL), D_FF ** -0.5),
        "g_final": 1.0 + nrm((D_MODEL,), 0.02),
    }


def reference(x, c, ctx, c_ctx, w_mod, b_mod, g_norm1, w_in, b_in, w_dw_a, ln_a_g, ln_a_b, w_out_a,
              w_dw_b, w_out_b, w_dw_c, b_dw_c, w_rg_a, b_rg_a, w_rg_x, b_rg_x, lam, w_out_c, w_o,
              g_norm2, w_up, w_dw_f, w_down, g_final):
    rows = x.shape[1] // GRID_W

    def conv_lat(t, w, pl, pr):
        return dwconv_rows(t, w, pl, pr, rows)

    def conv_lat_ffn(t, w, pl, pr):
        return dwconv_cols(t, w, pl, pr, rows)

    s_lat = jax.nn.silu(c)
    s_ctx = jax.nn.silu(c_ctx)
    h = x
    hc = ctx
    for l in range(DEPTH):
        last = l == DEPTH - 1
        mod_l = s_lat @ w_mod[l] + b_mod[l]
        mod_c = s_ctx @ w_mod[l] + b_mod[l]
        sh1, sc1, gt1, sh2, sc2, gt2 = jnp.split(mod_l[:, None, :], 6, axis=-1)
        csh1, csc1, cgt1, csh2, csc2, cgt2 = jnp.split(mod_c, 6)

        xl = modulate(rms_norm(h, g_norm1[l]), sh1, sc1)
        xc = modulate(rms_norm(hc, g_norm1[l]), csh1, csc1)
        parts_l = jnp.split(xl @ w_in[l] + b_in[l], IN_SPLITS, axis=-1)
        if last:
            v_ctx = xc @ w_in[l][:, LRU_X_OFF:LRU_X_OFF + D_LRU] + b_in[l][LRU_X_OFF:LRU_X_OFF + D_LRU]
            parts_c = None
        else:
            parts_c = jnp.split(xc @ w_in[l] + b_in[l], IN_SPLITS, axis=-1)
            v_ctx = parts_c[IDX_LRU_X]
        h_rec_l, h_rec_c = rglru_bidir(parts_l[IDX_LRU_X], v_ctx, w_dw_c[l], b_dw_c[l], w_rg_a[l],
                                       b_rg_a[l], w_rg_x[l], b_rg_x[l], lam[l], not last)
        h = h + gt1 * mixer_merge(parts_l, h_rec_l, conv_lat, w_dw_a[l], ln_a_g[l], ln_a_b[l],
                                  w_out_a[l], w_dw_b[l], w_out_b[l], w_out_c[l], w_o[l])
        yl = modulate(rms_norm(h, g_norm2[l]), sh2, sc2)
        h = h + gt2 * conv_ffn(yl, w_up[l], w_dw_f[l], w_down[l], conv_lat_ffn)

        if not last:
            hc = hc + cgt1 * mixer_merge(parts_c, h_rec_c, dwconv_line, w_dw_a[l], ln_a_g[l], ln_a_b[l],
                                         w_out_a[l], w_dw_b[l], w_out_b[l], w_out_c[l], w_o[l])
            yc = modulate(rms_norm(hc, g_norm2[l]), csh2, csc2)
            hc = hc + cgt2 * conv_ffn(yc, w_up[l], w_dw_f[l], w_down[l], dwconv_line)
    return rms_norm(h, g_final)
```

```python
import numpy as np
from contextlib import ExitStack
import concourse.bass as bass
import concourse.mybir as mybir
from concourse.bass_utils import run_bass_kernel_spmd

F32 = mybir.dt.float32
BF16 = mybir.dt.bfloat16
AF = mybir.ActivationFunctionType
ALU = mybir.AluOpType

D = 2048
KC = 16
S = 2048
CT = 256
T = 512
NTOK = 2 * S + 2 * CT
DFF = 6144
FC = 48
NIN = 20480
EPS = 1e-6
NCORES = 8
DEPTH = 2
SAME_SYNC = True
NW = 5
NTMP = 14

_off = {}
_c = 0
for _n, _w in [("bmod", 96), ("g1", 16), ("g2", 16), ("bin", 160), ("wdwa", 16 * 31), ("lng", 16), ("lnb", 16),
               ("wdwb", 48), ("wdwc", 64), ("bdwc", 16), ("brga", 32), ("brgx", 32), ("lam", 32), ("wdwf", 144)]:
    _off[_n] = _c
    _c += _w
NCOL = _c


def _pk(v):
    return np.ascontiguousarray(v.reshape(-1, 128).T)


def _pkt(w):
    t, n = w.shape
    return np.ascontiguousarray(w.T.reshape(n // 128, 128, t).transpose(1, 0, 2).reshape(128, -1))


class Op:
    __slots__ = ("fn", "waits", "clk", "inc")


class Prog:
    ENGS = ["pe", "act", "dve", "pool", "sp"]

    def __init__(self):
        self.streams = {e: [] for e in self.ENGS}
        self.clock = {}
        self.seen = {e: {} for e in self.ENGS}
        self.lastw = {}
        self.readers = {}

    def add(self, eng, fn, reads=(), writes=(), dma_key=None):
        deps = {}

        def need(c, v):
            if deps.get(c, 0) < v:
                deps[c] = v
        for r in reads:
            w = self.lastw.get(r)
            if w is not None:
                need(*w)
        for r in writes:
            w = self.lastw.get(r)
            if w is not None:
                need(*w)
            rd = self.readers.get(r)
            if rd:
                for c, v in rd.items():
                    need(c, v)
        if dma_key is None:
            clk = eng
            inc = 1
        else:
            clk = ("dma", dma_key)
            inc = 16
        val = self.clock.get(clk, 0) + inc
        self.clock[clk] = val
        waits = []
        seen = self.seen[eng]
        for c, v in deps.items():
            if c == eng and (eng == "pe" or not SAME_SYNC):
                continue
            if seen.get(c, 0) >= v:
                continue
            seen[c] = v
            waits.append((c, v))
        op = Op()
        op.fn = fn
        op.waits = waits
        op.clk = clk
        op.inc = inc
        self.streams[eng].append(op)
        me = (clk, val)
        for r in reads:
            d = self.readers.setdefault(r, {})
            if d.get(clk, 0) < val:
                d[clk] = val
        for r in writes:
            self.lastw[r] = me
            self.readers[r] = {}
        return me

    def barrier(self, exclude=()):
        snap = {c: v for c, v in self.clock.items() if c not in exclude}
        for eng in self.ENGS:
            waits = []
            seen = self.seen[eng]
            for c, v in snap.items():
                if c == eng:
                    continue
                if seen.get(c, 0) >= v:
                    continue
                seen[c] = v
                waits.append((c, v))
            if waits:
                op = Op()
                op.fn = None
                op.waits = waits
                op.clk = None
                op.inc = 0
                self.streams[eng].append(op)
        self.lastw = {}
        self.readers = {}


class Buf:
    def __init__(self, name, ap):
        self.name = name
        self.ap = ap

    def r(self, k=0):
        return (self.name, k)

    def rs(self, n):
        return [(self.name, k) for k in range(n)]


def build_program(debug=False):
    nc = bass.Bass("TRN2", target_bir_lowering=False)
    P = Prog()

    def dram_in(name, shape):
        return nc.dram_tensor(name, list(shape), F32, kind="ExternalInput").ap()

    xT = dram_in("xT", [D, NTOK])
    cmat = dram_in("cmat", [128, 48])
    pvec = dram_in("pvec", [DEPTH, 128, NCOL])
    gfin = dram_in("gfin", [128, 16])
    w_mod = dram_in("w_mod", [DEPTH, D, 6 * D])
    w_in = dram_in("w_in", [DEPTH, D, NIN])
    w_out_a = dram_in("w_out_a", [DEPTH, D, D])
    w_out_b = dram_in("w_out_b", [DEPTH, D, D])
    w_out_c = dram_in("w_out_c", [DEPTH, D, D])
    w_o = dram_in("w_o", [DEPTH, D, D])
    w_rg_a = dram_in("w_rg_a", [DEPTH, 4096, 256])
    w_rg_x = dram_in("w_rg_x", [DEPTH, 4096, 256])
    w_up = dram_in("w_up", [DEPTH, D, 2 * DFF])
    w_down = dram_in("w_down", [DEPTH, DFF, D])
    outT = nc.dram_tensor("outT", [D, 2 * S], F32, kind="ExternalOutput").ap()
    skind = "ExternalOutput" if debug else "Internal"
    hT = nc.dram_tensor("hT", [D, NTOK], F32, kind=skind).ap()
    h2T = nc.dram_tensor("h2T", [D, NTOK], F32, kind=skind).ap()
    vT = nc.dram_tensor("vT", [D, NTOK], F32, kind=skind).ap()
    hrT = nc.dram_tensor("hrT", [D, NTOK], F32, kind=skind).ap()
    NBLK = 184
    WBd = [nc.dram_tensor(f"WB{l_}", [NBLK, 128, 4096], BF16, kind="Internal").ap() for l_ in range(DEPTH)]

    def fm(ap):
        return ap.rearrange("(k p) t -> p k t", p=128)

    xT_v, hT_v, h2T_v, vT_v, hrT_v, outT_v = fm(xT), fm(hT), fm(h2T), fm(vT), fm(hrT), fm(outT)

    es = ExitStack()
    with es:
        def sb(name, shape, dt=F32):
            return es.enter_context(nc.sbuf_tensor(name, list(shape), dt))

        PV = sb("PV", [128, DEPTH, NCOL])
        GF = sb("GF", [128, 16])
        CM = sb("CM", [128, 48])
        SC3 = sb("SC3", [128, 16, 3])
        MOD = sb("MOD", [128, DEPTH, 96, 3])
        A1 = sb("A1", [128, DEPTH, 3, 16])
        A2 = sb("A2", [128, DEPTH, 3, 16])
        CL = sb("CL", [128, DEPTH, 32])
        CL2 = sb("CL2", [128, DEPTH, 32])
        ETMP = sb("ETMP", [128, DEPTH, 32])
        ONES = sb("ONES", [128, 128])
        IDB = sb("IDB", [128, 128], BF16)
        ONESB = sb("ONESB", [128, 128], BF16)
        IDF = sb("IDF", [128, 128])
        EPSC = sb("EPSC", [128, 1])
        ZERO = sb("ZERO", [128, 1])
        WSA = sb("WSA", [128, NW, 16, 256], BF16)
        WS = [WSA[:, i] for i in range(NW)]
        BIG = sb("BIG", [128, 12288])
        XB = sb("XB", [128, 16, 512], BF16)
        XH = sb("XH", [128, 16, 128], BF16)
        MB = sb("MB", [128, 16, 512], BF16)
        TMPA = sb("TMPA", [128, NTMP * 512])
        TMPS = [TMPA[:, i * 512:(i + 1) * 512] for i in range(NTMP)]
        MU = sb("MU", [128, 512])
        RSTD = sb("RSTD", [128, 512])
        RS = sb("RS", [128, 512])
        VAR = sb("VAR", [128, 512])
        DG31 = [sb(f"DG31_{i}", [128, 31, 128], BF16) for i in range(2)]
        DG4 = [sb(f"DG4_{i}", [128, 4, 128], BF16) for i in range(2)]
        YAP = [sb(f"YAP{i}", [128, 1024], BF16) for i in range(3)]
        GSB = [sb(f"GSB{i}", [128, 640], BF16) for i in range(3)]
        psum = [es.enter_context(nc.psum_tensor(f"ps{i}", [128, 512], F32)) for i in range(8)]

        BFv = BIG[:, 0:8192].rearrange("p (k t) -> p k t", k=16)
        Zv = BIG[:, 8192:12288].bitcast(BF16).rearrange("p (k t) -> p k t", k=16)
        HIDv = BIG[:, :].bitcast(BF16).rearrange("p (k t) -> p k t", k=48)
        HHv = MB[:, :, :].rearrange("p k t -> p (k t)").bitcast(F32)[:, 0:2048].rearrange("p (k t) -> p k t", k=16)
        BF = Buf("BF", BFv)
        Z = Buf("Z", Zv)
        HID = Buf("HID", HIDv)
        HH = Buf("HH", HHv)
        X = Buf("X", XB)
        XHb = Buf("XH", XH)
        MBb = Buf("MB", MB)

        st = {"wcount": 0, "tmp": 0, "ps": 0, "w": 0, "pinned": set(), "dg31": 0, "dg4": 0, "yap": 0, "gsb": 0}

        def tmp():
            i = st["tmp"]
            st["tmp"] = (i + 1) % NTMP
            return Buf(f"TMP{i}", TMPS[i])

        def ps():
            while True:
                i = st["ps"]
                st["ps"] = (i + 1) % 8
                if i not in st["pinned"]:
                    return Buf(f"PS{i}", psum[i])

        def wslot():
            i = st["w"]
            st["w"] = (i + 1) % NW
            return Buf(f"W{i}", WS[i])

        def rot(key, arr, nm):
            i = st[key]
            st[key] = (i + 1) % len(arr)
            return Buf(f"{nm}{i}", arr[i])

        def act(out, in_, func, reads, writes, bias=None, scale=1.0):
            b = ZERO[:, 0:1] if bias is None else bias
            P.add("act", lambda e: e.activation(out=out, in_=in_, func=func, bias=b, scale=scale), reads, writes)

        def tt(eng, out, a, b, op, reads, writes):
            P.add(eng, lambda e: e.tensor_tensor(out=out, in0=a, in1=b, op=op), reads, writes)

        def ts(eng, out, a, s1, s2, op0, op1, reads, writes):
            P.add(eng, lambda e: e.tensor_scalar(out=out, in0=a, scalar1=s1, scalar2=s2, op0=op0, op1=op1), reads, writes)

        def ts1(eng, out, a, s1, op0, reads, writes):
            P.add(eng, lambda e: e.tensor_scalar(out=out, in0=a, scalar1=s1, scalar2=None, op0=op0), reads, writes)

        def stt(out, in0, scalar, in1, op0, op1, reads, writes):
            P.add("dve", lambda e: e.scalar_tensor_tensor(out=out, in0=in0, scalar=scalar, in1=in1, op0=op0, op1=op1), reads, writes)

        def cp(eng, out, in_, reads, writes):
            P.add(eng, lambda e: e.tensor_copy(out=out, in_=in_), reads, writes)

        def mm(out, pairs, reads, writes, start=True, stop=True):
            def fn(e):
                n = len(pairs)
                ins = None
                for i, (l, r) in enumerate(pairs):
                    ins = e.matmul(out, l, r, start=(start and i == 0), stop=(stop and i == n - 1))
                return ins
            P.add("pe", fn, reads, writes)

        def dma(q, out, in_, reads, writes, key):
            P.add(q, lambda e: e.dma_start(out=out, in_=in_), reads, writes, dma_key=key)

        def memset(eng, ap, val, writes):
            P.add(eng, lambda e: e.memset(ap, val), (), writes)

        def wblk_src(l, idx):
            if idx < 80:
                return w_in[l][:, idx * 256:(idx + 1) * 256]
            if idx < 112:
                m = (w_out_a, w_out_b, w_out_c, w_o)[(idx - 80) // 8]
                jb = (idx - 80) % 8
                return m[l][:, jb * 256:(jb + 1) * 256]
            if idx < 160:
                jb = idx - 112
                return w_up[l][:, jb * 256:(jb + 1) * 256]
            kb, jb = divmod(idx - 160, 8)
            return w_down[l][kb * D:(kb + 1) * D, jb * 256:(jb + 1) * 256]

        def cast_block(l, idx, key):
            dma("pool", WBd[l][idx].rearrange("p (k m) -> p k m", k=16), wblk_src(l, idx).rearrange("(k p) m -> p k m", p=128),
                [], [("WB", l, key)], key=key)

        pending_casts = []

        def wload(l, idx, grp=None):
            if pending_casts and st["wcount"] % 3 == 0:
                pl, pidx, pkey = pending_casts.pop(0)
                cast_block(pl, pidx, pkey)
            st["wcount"] += 1
            w = wslot()
            key = "CAST0a" if (l == 0 and 40 <= idx < 48) else f"CAST{l}b"
            dma("sp", w.ap[:, :, :], WBd[l][idx].rearrange("p (k m) -> p k m", k=16), [("WB", l, key)], [w.r()], key=w.name)
            return w

        def pvc(l, name, j):
            o = _off[name] + j
            return PV[:, l, o:o + 1]

        dma("sp", PV[:, :, :], pvec.rearrange("l p c -> p l c"), [], [("PV", 0)], key="PV")
        dma("sp", GF[:, :], gfin, [], [("GF", 0)], key="GF")
        dma("sp", CM[:, :], cmat, [], [("CM", 0)], key="CM")
        memset("dve", ONES[:, :], 1.0, [("ONES", 0)])
        memset("dve", EPSC[:, :], EPS, [("EPSC", 0)])
        memset("dve", ZERO[:, :], 0.0, [("ZERO", 0)])
        memset("dve", IDF[:, :], 1.0, [("IDF", 0)])
        P.add("pool", lambda e: e.affine_select(out=IDF[:, :], in_=IDF[:, :], pattern=[[-1, 128]], compare_op=ALU.is_equal,
                                                fill=0.0, base=0, channel_multiplier=1), [("IDF", 0)], [("IDF", 0)])
        cp("dve", IDB[:, :], IDF[:, :], [("IDF", 0)], [("IDB", 0)])
        memset("dve", ONESB[:, :], 1.0, [("ONESB", 0)])
        for a in YAP:
            memset("dve", a[:, :], 0.0, [])
        for a in GSB:
            memset("dve", a[:, :], 0.0, [])
        P.barrier()

        for idx in range(40, 48):
            cast_block(0, idx, "CAST0a")
        rest0 = list(range(0, 40)) + list(range(48, NBLK))
        for idx in rest0[:100]:
            cast_block(0, idx, "CAST0b")
        late_casts0 = rest0[100:]

        act(SC3[:, :, :].rearrange("p k n -> p (k n)"), CM[:, :], AF.Silu, [("CM", 0)], [("SC3", 0)])
        WM = [BIG[:, 0:4096].rearrange("p (k m) -> p k m", k=16), BIG[:, 4096:8192].rearrange("p (k m) -> p k m", k=16)]
        for l in range(DEPTH):
            pm = ps()
            for jb in range(48):
                wi = jb % 2
                dma("sp", WM[wi], w_mod[l, :, jb * 256:(jb + 1) * 256].rearrange("(k p) m -> p k m", p=128),
                    [], [("WM", wi)], key=f"WM{wi}")
                for mc in range(2):
                    mi = jb * 2 + mc
                    mm(pm.ap[:, mi * 3:mi * 3 + 3],
                       [(WM[wi][:, k, mc * 128:(mc + 1) * 128], SC3[:, k, :]) for k in range(16)],
                       [("WM", wi), ("SC3", 0)], [pm.r()])
            o = _off["bmod"]
            tt("dve", MOD[:, l, :, :], pm.ap[:, 0:288].rearrange("p (j n) -> p j n", n=3),
               PV[:, l, o:o + 96].unsqueeze(2).broadcast_to([128, 96, 3]), ALU.add, [pm.r(), ("PV", 0)], [("MOD", l)])
            for r in range(3):
                stt(A1[:, l, r, :], MOD[:, l, 16:32, r], 1.0, PV[:, l, _off["g1"]:_off["g1"] + 16], ALU.add, ALU.mult,
                    [("MOD", l), ("PV", 0)], [("A1", l)])
                stt(A2[:, l, r, :], MOD[:, l, 64:80, r], 1.0, PV[:, l, _off["g2"]:_off["g2"] + 16], ALU.add, ALU.mult,
                    [("MOD", l), ("PV", 0)], [("A2", l)])
            o = _off["lam"]
            act(ETMP[:, l, :], PV[:, l, o:o + 32], AF.Exp, [("PV", 0)], [("ETMP", l)], scale=-1.0)
            ts1("dve", ETMP[:, l, :], ETMP[:, l, :], 1.0, ALU.add, [("ETMP", l)], [("ETMP", l)])
            act(ETMP[:, l, :], ETMP[:, l, :], AF.Ln, [("ETMP", l)], [("ETMP", l)])
            ts1("dve", CL[:, l, :], ETMP[:, l, :], -8.0, ALU.mult, [("ETMP", l)], [("CL", l)])
            ts1("dve", CL2[:, l, :], ETMP[:, l, :], -16.0, ALU.mult, [("ETMP", l)], [("CL2", l)])
        P.barrier(exclude={("dma", "CAST0b")})

        def modc(l, part, k, r):
            return MOD[:, l, part * 16 + k, r:r + 1]

        LAT = [(s * S + j * T, s, 8, 64, False, s, j) for s in range(2) for j in range(4)]
        CTXT = (2 * S, 2, 2, 256, True, -1, 0)

        def norm_p1(src, n, bank, k):
            sq = tmp()
            sqb = sq.ap.bitcast(BF16)
            act(sqb[:, :n], src.ap[:, k, :n], AF.Square, [src.r(k)], [sq.r()])
            mm(bank.ap[:, :n], [(ONESB[:, :], sqb[:, :n])], [sq.r(), ("ONESB", 0)], [bank.r()], start=(k == 0), stop=(k == 15))

        def norm_p2(src, n, bank, dst, dst_is_f32_store, Acol, Bcol, own_tmps=False):
            act(RS[:, :n], bank.ap[:, :n], AF.Sqrt, [bank.r()], [("RS", 0)], bias=EPSC[:, 0:1], scale=1.0 / D)
            P.add("dve", lambda e: e.reciprocal(out=RS[:, :n], in_=RS[:, :n]), [("RS", 0)], [("RS", 0)])
            for k in range(16):
                if own_tmps:
                    t1 = Buf("MU", MU) if k % 2 == 0 else Buf("VAR", VAR)
                else:
                    t1 = tmp()
                tt("dve", t1.ap[:, :n], src.ap[:, k, :n], RS[:, :n], ALU.mult, [src.r(k), ("RS", 0)], [t1.r()])
                if dst_is_f32_store is None:
                    act(dst.ap[:, k, :n], t1.ap[:, :n], AF.Identity, [t1.r()], [dst.r(k)], bias=Bcol(k), scale=Acol(k))
                else:
                    t2 = tmp()
                    act(t2.ap[:, :n], t1.ap[:, :n], AF.Identity, [t1.r()], [t2.r()], bias=Bcol(k), scale=Acol(k))
                    dst_is_f32_store(k, t2)

        def norm(src, n, dst, dst_is_f32_store, Acol, Bcol):
            bank = ps()
            for k in range(16):
                norm_p1(src, n, bank, k)
            norm_p2(src, n, bank, dst, dst_is_f32_store, Acol, Bcol)

        def proj_blocks(l, base, nblk, rhs_buf, consume, pre=None, la=3):
            nch = nblk * 2
            if pre is not None:
                for c0 in range(min(la, nch)):
                    pre(c0)
            for jb in range(nblk):
                w = wload(l, base + jb)
                for mc in range(2):
                    c = jb * 2 + mc
                    if pre is not None and c + la < nch:
                        pre(c + la)
                    pb = ps()
                    mm(pb.ap[:, :], [(w.ap[:, k, mc * 128:(mc + 1) * 128], rhs_buf.ap[:, k, :]) for k in range(16)],
                       [w.r()] + rhs_buf.rs(16), [pb.r()])
                    consume(c, pb)

        for l in range(DEPTH):
            last = (l == DEPTH - 1)
            src_v = xT_v if l == 0 else hT_v
            src_name = "xT" if l == 0 else "hT"
            Wl = w_in[l]
            tiles_all = LAT + [CTXT]
            tiles_main = LAT + ([] if last else [CTXT])

            for (tok0, mrow, nrow, R, isctx, sq_, j_) in tiles_all:
                dma("sp", BF.ap[:, :, :], src_v[:, :, tok0:tok0 + T], [(src_name, tok0)], BF.rs(16), key="BF")
                norm(BF, T, X, None, lambda k: A1[:, l, mrow, k:k + 1], lambda k: modc(l, 0, k, mrow))

                def cons_v(c, pb, tok0=tok0):
                    t1 = tmp()
                    act(t1.ap[:, :], pb.ap[:, :], AF.Identity, [pb.r()], [t1.r()], bias=pvc(l, "bin", 5 * 16 + c))
                    dma("sp" if l == 0 else "pool", vT_v[:, c, tok0:tok0 + T], t1.ap[:, :], [t1.r()], [("vT", c, tok0)], key=t1.name)
                proj_blocks(l, 40, 8, X, cons_v)
            P.barrier(exclude={("dma", "CAST0b")})

            LL = CT + S
            XBf = XB[:, :, :].rearrange("p k t -> p (k t)").bitcast(F32)
            MBf = MB[:, :, :].rearrange("p k t -> p (k t)").bitcast(F32)
            WSf = WSA[:, :, :, :].rearrange("p a k m -> p (a k m)").bitcast(F32)
            U32 = BIG[:, 0:2 * LL].rearrange("p (c t) -> p c t", c=2)
            Rs = [BIG[:, 2 * LL:3 * LL], XBf[:, 1024:1024 + LL]]
            Is = [BIG[:, 3 * LL:4 * LL], WSf[:, 0:LL]]
            As = [BIG[:, 4 * LL:5 * LL], WSf[:, LL:2 * LL]]
            Qs = [TMPA[:, 0:LL], WSf[:, 2 * LL:3 * LL]]
            H0 = TMPA[:, LL:2 * LL]
            PVb = TMPA[:, 2 * LL:2 * LL + 2320].bitcast(BF16).rearrange("p (c t) -> p c t", c=2)
            UBv = MBf[:, 0:LL].bitcast(BF16).rearrange("p (c t) -> p c t", c=2)
            WGs = [XBf[:, 0:1024].bitcast(BF16).rearrange("p (g m) -> p g m", g=8),
                   MBf[:, LL:LL + 1024].bitcast(BF16).rearrange("p (g m) -> p g m", g=8)]
            for c2 in range(2):
                memset("dve", PVb[:, c2, 0:2], 0.0, [("PVb", c2)])
                memset("dve", PVb[:, c2, 258:264], 0.0, [("PVb", c2)])
                memset("dve", PVb[:, c2, 2312:2320], 0.0, [("PVb", c2)])
            CO, LO = 0, 262
            heads = [(s_, hd_) for s_ in range(2) for hd_ in range(8)]

            def a2_loads(hi):
                s_, hd_ = heads[hi]
                WGn = WGs[hi % 2]
                for c2 in range(2):
                    c = hd_ * 2 + c2
                    dma("pool", PVb[:, c2, CO + 2:CO + 2 + CT], vT_v[:, c, 2 * S + s_ * CT: 2 * S + (s_ + 1) * CT],
                        [("vT", c, 2 * S)], [("PVb", c2)], key=f"PVbc{c2}")
                    dma("pool", PVb[:, c2, LO + 2:LO + 2 + S], vT_v[:, c, s_ * S:(s_ + 1) * S],
                        [("vT", c, s_ * S + jj * T) for jj in range(4)], [("PVb", c2)], key=f"PVbl{c2}")
                for gi, wsrc_ in enumerate((w_rg_a, w_rg_x)):
                    for d in range(2):
                        r0 = (d * 8 + hd_) * 256
                        dma("pool", WGn[:, gi * 4 + d * 2: gi * 4 + d * 2 + 2, :],
                            wsrc_[l, r0:r0 + 256, :].rearrange("(i p) m -> p i m", p=128), [], [("WG", hi % 2, gi * 2 + d)],
                            key=f"WG{hi % 2}{gi}{d}")
            a2_loads(0)
            for hi, (s, hd) in enumerate(heads):
                WG = WGs[hi % 2]
                for c2 in range(2):
                    c = hd * 2 + c2
                    dg = rot("dg4", DG4, "DG4_")
                    ow = _off["wdwc"] + c * 4
                    tt("dve", dg.ap[:, :, :], IDF[:, :].unsqueeze(1).broadcast_to([128, 4, 128]),
                       PV[:, l, ow:ow + 4].unsqueeze(2).broadcast_to([128, 4, 128]), ALU.mult, [("IDF", 0), ("PV", 0)], [dg.r()])
                    segs = [(CO, 0, CT)] + [(LO + jj * T, CT + jj * T, T) for jj in range(4)]
                    for (po, uo, n) in segs:
                        pb = ps()
                        mm(pb.ap[:, :n], [(dg.ap[:, k, :], PVb[:, c2, po + k:po + k + n]) for k in range(4)],
                           [dg.r(), ("PVb", c2)], [pb.r()])
                        act(U32[:, c2, uo:uo + n], pb.ap[:, :n], AF.Identity, [pb.r()], [("U32", c2)], bias=pvc(l, "bdwc", c))
                    cp("pool", UBv[:, c2, :], U32[:, c2, :], [("U32", c2)], [("UB", c2)])
                if hi + 1 < len(heads):
                    a2_loads(hi + 1)
                if l == 0:
                    for _ in range(6):
                        if late_casts0:
                            cast_block(0, late_casts0.pop(0), "CAST0b")
                for mo in range(2):
                    c = hd * 2 + mo
                    segs = [(0, CT)] + [(CT + jj * T, T) for jj in range(4)]
                    for d in range(2):
                        Rb, Ib, Ab, Qb = Rs[d], Is[d], As[d], Qs[d]
                        for gi, dstb, bname, rn in ((0, Rb, "brga", "R"), (1, Ib, "brgx", "I")):
                            for (uo, n) in segs:
                                pb = ps()
                                mm(pb.ap[:, :n], [(WG[:, gi * 4 + d * 2 + ic, mo * 128:(mo + 1) * 128], UBv[:, ic, uo:uo + n]) for ic in range(2)],
                                   [("WG", hi % 2, gi * 2 + d), ("UB", 0), ("UB", 1)], [pb.r()])
                                act(dstb[:, uo:uo + n], pb.ap[:, :n], AF.Sigmoid, [pb.r()], [(rn, d)],
                                    bias=pvc(l, bname, d * 16 + c))
                        tt("pool", Ib[:, :], Ib[:, :], U32[:, mo, :], ALU.mult, [("I", d), ("U32", mo)], [("I", d)])
                        act(Ab[:, :], Rb[:, :], AF.Exp, [("R", d)], [("A", d)], scale=CL[:, l, d * 16 + c: d * 16 + c + 1])
                        tt("dve", Qb[:, :], Ab[:, :], Ab[:, :], ALU.mult, [("A", d)], [("Q", d)])
                        act(Qb[:, :], Qb[:, :], AF.Sqrt, [("Q", d)], [("Q", d)], bias=ONES[:, 0:1], scale=-1.0)
                        tt("pool", Qb[:, :], Qb[:, :], Ib[:, :], ALU.mult, [("Q", d), ("I", d)], [("Q", d)])
                        if d == 0:
                            P.add("dve", lambda e, Ab=Ab, Qb=Qb: e.tensor_tensor_scan(
                                out=H0[:, 0:CT], data0=Ab[:, 0:CT], data1=Qb[:, 0:CT], initial=0.0, op0=ALU.mult, op1=ALU.add),
                                [("A", d), ("Q", d)], [("H0", 0)])
                            P.add("dve", lambda e, Ab=Ab, Qb=Qb: e.tensor_tensor_scan(
                                out=H0[:, CT:LL], data0=Ab[:, CT:LL], data1=Qb[:, CT:LL], initial=H0[:, CT - 1:CT], op0=ALU.mult, op1=ALU.add),
                                [("A", d), ("Q", d), ("H0", 0)], [("H0", 0)])
                        else:
                            P.add("dve", lambda e, Ab=Ab, Qb=Qb, Rb=Rb: e.tensor_tensor_scan(
                                out=Rb[:, 0:CT][:, ::-1], data0=Ab[:, 0:CT][:, ::-1], data1=Qb[:, 0:CT][:, ::-1], initial=0.0, op0=ALU.mult, op1=ALU.add),
                                [("A", d), ("Q", d), ("R", d)], [("R", d)])
                            P.add("dve", lambda e, Ab=Ab, Qb=Qb, Rb=Rb: e.tensor_tensor_scan(
                                out=Rb[:, CT:LL][:, ::-1], data0=Ab[:, CT:LL][:, ::-1], data1=Qb[:, CT:LL][:, ::-1], initial=Rb[:, 0:1], op0=ALU.mult, op1=ALU.add),
                                [("A", d), ("Q", d), ("R", d)], [("R", d)])
                    Io = Is[1]
                    tt("dve", Io[:, :], H0[:, :], Rs[1][:, :], ALU.add, [("H0", 0), ("R", 1), ("I", 1)], [("I", 1)])
                    dma("pool", hrT_v[:, c, s * S:(s + 1) * S], Io[:, CT:LL], [("I", 1)], [("hrT", c, s * S + jj * T) for jj in range(4)], key="HSl")
                    if not last:
                        dma("pool", hrT_v[:, c, 2 * S + s * CT:2 * S + (s + 1) * CT], Io[:, 0:CT], [("I", 1)], [("hrT", c, 2 * S)], key="HSc")
            P.barrier()

            if l + 1 < DEPTH:
                for idx in range(NBLK):
                    pending_casts.append((l + 1, idx, f"CAST{l + 1}b"))
            for ti_, (tok0, mrow, nrow, R, isctx, sq_, j_) in enumerate(tiles_main):
                RP = R + 30
                if ti_ == 0:
                    dma("sp", BF.ap[:, :, :], src_v[:, :, tok0:tok0 + T], [(src_name, tok0)], BF.rs(16), key="BF")
                    norm(BF, T, X, None, lambda k: A1[:, l, mrow, k:k + 1], lambda k: modc(l, 0, k, mrow))
                s1 = ps()
                st["pinned"].add(int(s1.name[2:]))
                s2 = ps()
                st["pinned"].add(int(s2.name[2:]))
                wva, wga = {}, {}

                def a_stage1(c):
                    jb, mc = divmod(c, 2)
                    if mc == 0:
                        wva[jb] = wload(l, 0 * 8 + jb)
                        wga[jb] = wload(l, 1 * 8 + jb)
                    wv_, wg_ = wva[jb], wga[jb]
                    pa = ps()
                    mm(pa.ap[:, :], [(wv_.ap[:, k, mc * 128:(mc + 1) * 128], X.ap[:, k, :]) for k in range(16)], [wv_.r()] + X.rs(16), [pa.r()])
                    pg = ps()
                    mm(pg.ap[:, :], [(wg_.ap[:, k, mc * 128:(mc + 1) * 128], X.ap[:, k, :]) for k in range(16)], [wg_.r()] + X.rs(16), [pg.r()])
                    sg = tmp()
                    act(sg.ap[:, :], pg.ap[:, :], AF.Sigmoid, [pg.r()], [sg.r()], bias=pvc(l, "bin", 16 + c))
                    ya = rot("yap", YAP, "YAP")
                    yav = ya.ap[:, 0:nrow * RP].rearrange("p (r c) -> p r c", r=nrow)
                    stt(yav[:, :, 15:15 + R], pa.ap[:, :].rearrange("p (r c) -> p r c", r=nrow), pvc(l, "bin", c),
                        sg.ap[:, :].rearrange("p (r c) -> p r c", r=nrow), ALU.add, ALU.mult, [pa.r(), sg.r()], [ya.r()])
                    dg = rot("dg31", DG31, "DG31_")
                    ow = _off["wdwa"] + c * 31
                    tt("dve", dg.ap[:, :, :], IDF[:, :].unsqueeze(1).broadcast_to([128, 31, 128]),
                       PV[:, l, ow:ow + 31].unsqueeze(2).broadcast_to([128, 31, 128]), ALU.mult, [("IDF", 0), ("PV", 0)], [dg.r()])
                    return (ya, yav, dg)

                def a_stage2(c, stt_):
                    ya, yav, dg = stt_
                    pc_ = ps()
                    mm(pc_.ap[:, :].rearrange("p (r c) -> p r c", r=nrow), [(dg.ap[:, k, :], yav[:, :, k:k + R]) for k in range(31)],
                       [dg.r(), ya.r()], [pc_.r()])
                    act(BF.ap[:, c, :], pc_.ap[:, :], AF.Identity, [pc_.r()], [BF.r(c)])
                    sq = tmp()
                    sqb = sq.ap.bitcast(BF16)
                    tt("dve", sqb[:, 0:T], BF.ap[:, c, :], BF.ap[:, c, :], ALU.mult, [BF.r(c)], [sq.r()])
                    mm(s1.ap[:, :], [(ONES[:, :], BF.ap[:, c, :])], [BF.r(c)], [s1.r()], start=(c == 0), stop=(c == 15))
                    mm(s2.ap[:, :], [(ONESB[:, :], sqb[:, 0:T])], [sq.r()], [s2.r()], start=(c == 0), stop=(c == 15))
                cur = a_stage1(0)
                for c in range(16):
                    nxt_s = a_stage1(c + 1) if c + 1 < 16 else None
                    a_stage2(c, cur)
                    cur = nxt_s
                ts1("dve", MU[:, :], s1.ap[:, :], 1.0 / D, ALU.mult, [s1.r()], [("MU", 0)])
                tt("dve", VAR[:, :], MU[:, :], MU[:, :], ALU.mult, [("MU", 0)], [("VAR", 0)])
                stt(VAR[:, :], s2.ap[:, :], 1.0 / D, VAR[:, :], ALU.mult, ALU.subtract, [s2.r(), ("VAR", 0)], [("VAR", 0)])
                act(RSTD[:, :], VAR[:, :], AF.Sqrt, [("VAR", 0)], [("RSTD", 0)], bias=EPSC[:, 0:1])
                P.add("dve", lambda e: e.reciprocal(out=RSTD[:, :], in_=RSTD[:, :]), [("RSTD", 0)], [("RSTD", 0)])
                st["pinned"].discard(int(s1.name[2:]))
                st["pinned"].discard(int(s2.name[2:]))

                def ga_half(h):
                    for jb in range(4 * h, 4 * h + 4):
                        w2 = wload(l, 7 * 8 + jb)
                        for mc in range(2):
                            c = jb * 2 + mc
                            p2 = ps()
                            mm(p2.ap[:, :], [(w2.ap[:, k, mc * 128:(mc + 1) * 128], X.ap[:, k, :]) for k in range(16)], [w2.r()] + X.rs(16), [p2.r()])
                            act(MBb.ap[:, c, :], p2.ap[:, :], AF.Sigmoid, [p2.r()], [MBb.r(c)], bias=pvc(l, "bin", 7 * 16 + c))
                ga_half(0)
                for c in range(16):
                    t1 = tmp()
                    tt("dve", t1.ap[:, :], BF.ap[:, c, :], MU[:, :], ALU.subtract, [BF.r(c), ("MU", 0)], [t1.r()])
                    tt("dve", t1.ap[:, :], t1.ap[:, :], RSTD[:, :], ALU.mult, [t1.r(), ("RSTD", 0)], [t1.r()])
                    act(Z.ap[:, c, :], t1.ap[:, :], AF.Silu, [t1.r()], [Z.r(c)], bias=pvc(l, "lnb", c), scale=pvc(l, "lng", c))
                ga_half(1)
                for jb in range(8):
                    w1 = wload(l, 80 + jb)
                    for mc in range(2):
                        c = jb * 2 + mc
                        p1 = ps()
                        mm(p1.ap[:, :], [(w1.ap[:, k, mc * 128:(mc + 1) * 128], Z.ap[:, k, :]) for k in range(16)], [w1.r()] + Z.rs(16), [p1.r()])
                        tt("dve", BF.ap[:, c, :], p1.ap[:, :], MBb.ap[:, c, :], ALU.mult, [p1.r(), MBb.r(c)], [BF.r(c)])


                def outproj(wout, gseg, first, lastb):
                    for jb in range(8):
                        w1 = wload(l, wout + jb)
                        w2 = wload(l, gseg * 8 + jb)
                        for mc in range(2):
                            c = jb * 2 + mc
                            p1 = ps()
                            mm(p1.ap[:, :], [(w1.ap[:, k, mc * 128:(mc + 1) * 128], Z.ap[:, k, :]) for k in range(16)], [w1.r()] + Z.rs(16), [p1.r()])
                            p2 = ps()
                            mm(p2.ap[:, :], [(w2.ap[:, k, mc * 128:(mc + 1) * 128], X.ap[:, k, :]) for k in range(16)], [w2.r()] + X.rs(16), [p2.r()])
                            sg = tmp()
                            act(sg.ap[:, :], p2.ap[:, :], AF.Sigmoid, [p2.r()], [sg.r()], bias=pvc(l, "bin", gseg * 16 + c))
                            if first:
                                tt("dve", BF.ap[:, c, :], p1.ap[:, :], sg.ap[:, :], ALU.mult, [p1.r(), sg.r()], [BF.r(c)])
                            else:
                                tt("dve", sg.ap[:, :], p1.ap[:, :], sg.ap[:, :], ALU.mult, [p1.r(), sg.r()], [sg.r()])
                                if lastb:
                                    tt("dve", MBb.ap[:, c, :], BF.ap[:, c, :], sg.ap[:, :], ALU.add, [BF.r(c), sg.r()], [MBb.r(c)])
                                else:
                                    tt("dve", BF.ap[:, c, :], BF.ap[:, c, :], sg.ap[:, :], ALU.add, [BF.r(c), sg.r()], [BF.r(c)])
                R2 = R + 2
                wcs, whs, wbs = {}, {}, {}

                def b_stage1(c):
                    jb, mc = divmod(c, 2)
                    if mc == 0:
                        wcs[jb] = wload(l, 3 * 8 + jb)
                        whs[jb] = wload(l, 4 * 8 + jb)
                        wbs[jb] = wload(l, 2 * 8 + jb)
                    wc_, wh_, wb_ = wcs[jb], whs[jb], wbs[jb]
                    pcc = ps()
                    mm(pcc.ap[:, :], [(wc_.ap[:, k, mc * 128:(mc + 1) * 128], X.ap[:, k, :]) for k in range(16)], [wc_.r()] + X.rs(16), [pcc.r()])
                    ph = ps()
                    mm(ph.ap[:, :], [(wh_.ap[:, k, mc * 128:(mc + 1) * 128], X.ap[:, k, :]) for k in range(16)], [wh_.r()] + X.rs(16), [ph.r()])
                    pbb = ps()
                    mm(pbb.ap[:, :], [(wb_.ap[:, k, mc * 128:(mc + 1) * 128], X.ap[:, k, :]) for k in range(16)], [wb_.r()] + X.rs(16), [pbb.r()])
                    scb = tmp()
                    act(scb.ap[:, :], pcc.ap[:, :], AF.Identity, [pcc.r()], [scb.r()], bias=pvc(l, "bin", 3 * 16 + c))
                    ya = rot("yap", YAP, "YAP")
                    yav = ya.ap[:, 0:nrow * R2].rearrange("p (r c) -> p r c", r=nrow)
                    memset("dve", yav[:, :, 0:1], 0.0, [ya.r()])
                    memset("dve", yav[:, :, R + 1:R + 2], 0.0, [ya.r()])
                    stt(yav[:, :, 1:1 + R], ph.ap[:, :].rearrange("p (r c) -> p r c", r=nrow), pvc(l, "bin", 4 * 16 + c),
                        scb.ap[:, :].rearrange("p (r c) -> p r c", r=nrow), ALU.add, ALU.mult, [ph.r(), scb.r()], [ya.r()])
                    dg = rot("dg4", DG4, "DG4_")
                    ow = _off["wdwb"] + c * 3
                    tt("dve", dg.ap[:, 0:3, :], IDF[:, :].unsqueeze(1).broadcast_to([128, 3, 128]),
                       PV[:, l, ow:ow + 3].unsqueeze(2).broadcast_to([128, 3, 128]), ALU.mult, [("IDF", 0), ("PV", 0)], [dg.r()])
                    sbb = tmp()
                    act(sbb.ap[:, :], pbb.ap[:, :], AF.Identity, [pbb.r()], [sbb.r()], bias=pvc(l, "bin", 2 * 16 + c))
                    return (ya, yav, dg, sbb)

                def b_stage2(c, stt_):
                    ya, yav, dg, sbb = stt_
                    pcv = ps()
                    mm(pcv.ap[:, :].rearrange("p (r c) -> p r c", r=nrow), [(dg.ap[:, k, :], yav[:, :, k:k + R]) for k in range(3)],
                       [dg.r(), ya.r()], [pcv.r()])
                    tt("dve", Z.ap[:, c, :], pcv.ap[:, :], sbb.ap[:, :], ALU.mult, [pcv.r(), sbb.r()], [Z.r(c)])
                cur = b_stage1(0)
                for c in range(16):
                    nxt_s = b_stage1(c + 1) if c + 1 < 16 else None
                    b_stage2(c, cur)
                    cur = nxt_s
                for a_i, a in enumerate(YAP):
                    memset("dve", a[:, :], 0.0, [(f"YAP{a_i}", 0)])
                outproj(88, 8, False, False)

                hrs = {}

                def pre_c(c, tok0=tok0, hrs=hrs):
                    hr = tmp()
                    dma("sp", hr.ap[:, :], hrT_v[:, c, tok0:tok0 + T], [("hrT", c, tok0)], [hr.r()], key=hr.name)
                    hrs[c] = hr

                def cons_c(c, pb, tok0=tok0, hrs=hrs):
                    gl = tmp()
                    act(gl.ap[:, :], pb.ap[:, :], AF.Gelu_apprx_tanh, [pb.r()], [gl.r()], bias=pvc(l, "bin", 6 * 16 + c))
                    hr = hrs.pop(c)
                    tt("dve", Z.ap[:, c, :], gl.ap[:, :], hr.ap[:, :], ALU.mult, [gl.r(), hr.r()], [Z.r(c)])
                proj_blocks(l, 48, 8, X, cons_c, pre=pre_c)
                outproj(96, 9, False, True)

                hcs = {}
                nxt = tiles_main[ti_ + 1] if ti_ + 1 < len(tiles_main) else None
                nbank = None
                if nxt is not None:
                    ntok0, nmrow = nxt[0], nxt[1]
                    dma("sp", BF.ap[:, :, :], src_v[:, :, ntok0:ntok0 + T], [(src_name, ntok0)], BF.rs(16), key="BF")
                    nbank = ps()
                    st["pinned"].add(int(nbank.name[2:]))

                def pre_o(c, tok0=tok0, hcs=hcs):
                    hc = tmp()
                    dma("sp", hc.ap[:, :], src_v[:, c, tok0:tok0 + T], [(src_name, tok0)], [hc.r()], key=hc.name)
                    hcs[c] = hc

                def cons_o(c, pb, tok0=tok0, mrow=mrow, hcs=hcs, nbank=nbank):
                    hc = hcs.pop(c)
                    stt(hc.ap[:, :], pb.ap[:, :], modc(l, 2, c, mrow), hc.ap[:, :], ALU.mult, ALU.add, [pb.r(), hc.r()], [hc.r()])
                    dma("pool", h2T_v[:, c, tok0:tok0 + T], hc.ap[:, :], [hc.r()], [("h2T", tok0)], key=hc.name)
                    if nbank is not None:
                        if c < 8:
                            norm_p1(BF, T, nbank, 2 * c)
                            norm_p1(BF, T, nbank, 2 * c + 1)
                        if c == 8:
                            st["pinned"].discard(int(nbank.name[2:]))
                            norm_p2(BF, T, nbank, X, None, lambda k, nmrow=nmrow: A1[:, l, nmrow, k:k + 1], lambda k, nmrow=nmrow: modc(l, 0, k, nmrow), own_tmps=True)
                proj_blocks(l, 104, 8, MBb, cons_o, pre=pre_o)
            P.barrier()

            for (tok0, mrow, nrow, R, isctx, sq_, j_) in tiles_main:
                has_prev = (not isctx) and j_ > 0
                has_next = (not isctx) and j_ < 3
                dma("sp", BF.ap[:, :, :], h2T_v[:, :, tok0:tok0 + T], [("h2T", tok0)], BF.rs(16) + HID.rs(48), key="BF")
                Acol = lambda k: A2[:, l, mrow, k:k + 1]
                Bcol = lambda k: modc(l, 3, k, mrow)
                norm(BF, T, X, None, Acol, Bcol)
                if has_prev:
                    dma("sp", HH.ap[:, :, 0:64], h2T_v[:, :, tok0 - 64:tok0], [("h2T", tok0 - T)], [HH.r(0)], key="HH0")
                if has_next:
                    dma("sp", HH.ap[:, :, 64:128], h2T_v[:, :, tok0 + T:tok0 + T + 64], [("h2T", tok0 + T)], [HH.r(0)], key="HH1")
                h0 = 0 if has_prev else 64
                h1 = 128 if has_next else 64
                nh = h1 - h0
                if nh > 0:
                    HHs = Buf("HH", HH.ap[:, :, h0:h1])
                    XHs = Buf("XH", XHb.ap[:, :, h0:h1])
                    HHs.r = lambda k=0: ("HH", 0)
                    XHs.r = lambda k=0: ("XH", 0)
                    norm(HHs, nh, XHs, None, Acol, Bcol)
                wgs, wvs = {}, {}

                def b2_stage1(c):
                    jb, mc = divmod(c, 2)
                    if mc == 0:
                        wgs[jb] = wload(l, 112 + jb)
                        wvs[jb] = wload(l, 136 + jb)
                    wg_ = wgs[jb]
                    pa = ps()
                    mm(pa.ap[:, :], [(wg_.ap[:, k, mc * 128:(mc + 1) * 128], X.ap[:, k, :]) for k in range(16)], [wg_.r()] + X.rs(16), [pa.r()])
                    g = rot("gsb", GSB, "GSB")
                    ow = _off["wdwf"] + c * 3
                    dg = rot("dg4", DG4, "DG4_")
                    tt("dve", dg.ap[:, 0:3, :], IDF[:, :].unsqueeze(1).broadcast_to([128, 3, 128]),
                       PV[:, l, ow:ow + 3].unsqueeze(2).broadcast_to([128, 3, 128]), ALU.mult, [("IDF", 0), ("PV", 0)], [dg.r()])
                    if isctx:
                        gv = g.ap[:, 0:2 * 258].rearrange("p (r c) -> p r c", r=2)
                        memset("dve", gv[:, :, 0:1], 0.0, [g.r()])
                        memset("dve", gv[:, :, 257:258], 0.0, [g.r()])
                        act(gv[:, :, 1:257], pa.ap[:, :].rearrange("p (r c) -> p r c", r=2), AF.Identity, [pa.r()], [g.r()])
                    else:
                        if nh > 0:
                            phh = ps()
                            mm(phh.ap[:, 0:nh], [(wg_.ap[:, k, mc * 128:(mc + 1) * 128], XHb.ap[:, k, h0:h1]) for k in range(16)],
                               [wg_.r(), ("XH", 0)], [phh.r()])
                        if has_prev:
                            cp("dve", g.ap[:, 0:64], phh.ap[:, 0:64], [phh.r()], [g.r()])
                        else:
                            memset("dve", g.ap[:, 0:64], 0.0, [g.r()])
                        if has_next:
                            cp("dve", g.ap[:, 576:640], phh.ap[:, 64 - h0:128 - h0], [phh.r()], [g.r()])
                        else:
                            memset("dve", g.ap[:, 576:640], 0.0, [g.r()])
                        act(g.ap[:, 64:576], pa.ap[:, :], AF.Identity, [pa.r()], [g.r()])
                    return (g, dg)

                def b2_stage2(c, stt_):
                    g, dg = stt_
                    jb, mc = divmod(c, 2)
                    wv_ = wvs[jb]
                    pcv = ps()
                    if isctx:
                        gv = g.ap[:, 0:2 * 258].rearrange("p (r c) -> p r c", r=2)
                        mm(pcv.ap[:, :].rearrange("p (r c) -> p r c", r=2), [(dg.ap[:, k, :], gv[:, :, k:k + 256]) for k in range(3)],
                           [dg.r(), g.r()], [pcv.r()])
                    else:
                        mm(pcv.ap[:, :], [(dg.ap[:, k, :], g.ap[:, 64 * k:64 * k + 512]) for k in range(3)], [dg.r(), g.r()], [pcv.r()])
                    gg = tmp()
                    act(gg.ap[:, :], pcv.ap[:, :], AF.Gelu_apprx_tanh, [pcv.r()], [gg.r()])
                    pv_ = ps()
                    mm(pv_.ap[:, :], [(wv_.ap[:, k, mc * 128:(mc + 1) * 128], X.ap[:, k, :]) for k in range(16)], [wv_.r()] + X.rs(16), [pv_.r()])
                    tt("dve", HID.ap[:, c, :], pv_.ap[:, :], gg.ap[:, :], ALU.mult, [pv_.r(), gg.r()], [HID.r(c)] + ([BF.r(c // 2)] if c < 32 else []))
                cur = b2_stage1(0)
                for c in range(48):
                    nxt_s = b2_stage1(c + 1) if c + 1 < 48 else None
                    b2_stage2(c, cur)
                    cur = nxt_s
                hcs = {}

                def pre_d(c, tok0=tok0, hcs=hcs):
                    hc = tmp()
                    dma("sp", hc.ap[:, :], h2T_v[:, c, tok0:tok0 + T], [("h2T", tok0)], [hc.r()], key=hc.name)
                    hcs[c] = hc
                for c0 in range(3):
                    pre_d(c0)
                for jb in range(8):
                    wk = [wload(l, 160 + kb * 8 + jb) for kb in range(3)]
                    for mc in range(2):
                        c = jb * 2 + mc
                        if c + 3 < 16:
                            pre_d(c + 3)
                        pb = ps()
                        pairs = []
                        for kb in range(3):
                            pairs += [(wk[kb].ap[:, k, mc * 128:(mc + 1) * 128], HID.ap[:, kb * 16 + k, :]) for k in range(16)]
                        mm(pb.ap[:, :], pairs, [w_.r() for w_ in wk] + HID.rs(48), [pb.r()])
                        hc = hcs.pop(c)
                        stt(hc.ap[:, :], pb.ap[:, :], modc(l, 5, c, mrow), hc.ap[:, :], ALU.mult, ALU.add, [pb.r(), hc.r()], [hc.r()])
                        dma("pool", hT_v[:, c, tok0:tok0 + T], hc.ap[:, :], [hc.r()], [("hT", tok0)], key=hc.name)
            P.barrier()

        for (tok0, mrow, nrow, R, isctx, sq_, j_) in LAT:
            dma("sp", BF.ap[:, :, :], hT_v[:, :, tok0:tok0 + T], [("hT", tok0)], BF.rs(16), key="BF")

            def store(k, t2, tok0=tok0):
                dma("pool", outT_v[:, k, tok0:tok0 + T], t2.ap[:, :], [t2.r()], [("outT", k, tok0)], key=t2.name)
            norm(BF, T, None, store, lambda k: GF[:, k:k + 1], lambda k: ZERO[:, 0:1])
        P.barrier()

        sems = {}
        for clk in P.clock:
            nm = "s_" + "".join(ch for ch in str(clk) if ch.isalnum())
            sems[clk] = es.enter_context(nc.semaphore(nm))
        block = es.enter_context(nc.Block())
        bname = {"pe": "tensor", "act": "scalar", "dve": "vector", "pool": "gpsimd", "sp": "sync"}

        def make(engname):
            def body(e):
                for op in P.streams[engname]:
                    for (c, v) in op.waits:
                        e.wait_ge(sems[c], v)
                    if op.fn is not None:
                        ins = op.fn(e)
                        ins.then_inc(sems[op.clk], op.inc)
            return body
        for engname in Prog.ENGS:
            getattr(block, bname[engname])(make(engname))
    return nc


_NC_CACHE = {}


def _prep_inputs(inp):
    f = lambda a: np.ascontiguousarray(np.asarray(a, dtype=np.float32))
    x, c, ctx, c_ctx = f(inp["x"]), f(inp["c"]), f(inp["ctx"]), f(inp["c_ctx"])
    pv = np.zeros((DEPTH, 128, NCOL), np.float32)
    for l in range(DEPTH):
        def put(name, arr):
            pv[l, :, _off[name]:_off[name] + arr.shape[1]] = arr
        put("bmod", _pk(f(inp["b_mod"])[l]))
        put("g1", _pk(f(inp["g_norm1"])[l]))
        put("g2", _pk(f(inp["g_norm2"])[l]))
        put("bin", _pk(f(inp["b_in"])[l]))
        put("wdwa", _pkt(f(inp["w_dw_a"])[l]))
        put("lng", _pk(f(inp["ln_a_g"])[l]))
        put("lnb", _pk(f(inp["ln_a_b"])[l]))
        put("wdwb", _pkt(f(inp["w_dw_b"])[l]))
        put("wdwc", _pkt(f(inp["w_dw_c"])[l]))
        put("bdwc", _pk(f(inp["b_dw_c"])[l]))
        put("brga", _pk(f(inp["b_rg_a"])[l].reshape(-1)))
        put("brgx", _pk(f(inp["b_rg_x"])[l].reshape(-1)))
        put("lam", _pk(f(inp["lam"])[l].reshape(-1)))
        put("wdwf", _pkt(f(inp["w_dw_f"])[l]))
    shared = {
        "pvec": pv, "gfin": _pk(f(inp["g_final"])),
        "w_mod": f(inp["w_mod"]), "w_in": f(inp["w_in"]), "w_out_a": f(inp["w_out_a"]), "w_out_b": f(inp["w_out_b"]),
        "w_out_c": f(inp["w_out_c"]), "w_o": f(inp["w_o"]),
        "w_rg_a": f(inp["w_rg_a"]).reshape(DEPTH, 4096, 256), "w_rg_x": f(inp["w_rg_x"]).reshape(DEPTH, 4096, 256),
        "w_up": f(inp["w_up"]), "w_down": f(inp["w_down"]),
    }
    in_maps = []
    for i in range(NCORES):
        b0, b1 = 2 * i, 2 * i + 1
        xT = np.ascontiguousarray(np.concatenate([x[b0].T, x[b1].T, ctx[b0].T, ctx[b1].T], axis=1))
        cm = np.stack([c[b0], c[b1], c_ctx], axis=1)
        cm = np.ascontiguousarray(cm.reshape(16, 128, 3).transpose(1, 0, 2).reshape(128, 48))
        m = dict(shared)
        m["xT"] = xT
        m["cmat"] = cm
        in_maps.append(m)
    return in_maps


def kernel(**inputs):
    in_maps = _prep_inputs(inputs)
    if "nc" not in _NC_CACHE:
        _NC_CACHE["nc"] = build_program()
    nc = _NC_CACHE["nc"]
    res = run_bass_kernel_spmd(nc, in_maps, core_ids=list(range(NCORES)))
    out = np.empty((2 * NCORES, S, D), np.float32)
    for i in range(NCORES):
        oT = res.results[i]["outT"]
        out[2 * i] = oT[:, 0:S].T
        out[2 * i + 1] = oT[:, S:2 * S].T
    return out
```

```python
import numpy as np
from contextlib import ExitStack
import concourse.bass as bass
import concourse.mybir as mybir
from concourse.bass_utils import run_bass_kernel_spmd

F32 = mybir.dt.float32
BF16 = mybir.dt.bfloat16
AF = mybir.ActivationFunctionType
ALU = mybir.AluOpType

D = 2048
KC = 16
S = 2048
CT = 256
T = 512
NTOK = 2 * S + 2 * CT
DFF = 6144
FC = 48
NIN = 20480
EPS = 1e-6
NCORES = 8
DEPTH = 2
SAME_SYNC = True
NW = 5
NTMP = 14

_off = {}
_c = 0
for _n, _w in [("bmod", 96), ("g1", 16), ("g2", 16), ("bin", 160), ("wdwa", 16 * 31), ("lng", 16), ("lnb", 16),
               ("wdwb", 48), ("wdwc", 64), ("bdwc", 16), ("brga", 32), ("brgx", 32), ("lam", 32), ("wdwf", 144)]:
    _off[_n] = _c
    _c += _w
NCOL = _c


def _pk(v):
    return np.ascontiguousarray(v.reshape(-1, 128).T)


def _pkt(w):
    t, n = w.shape
    return np.ascontiguousarray(w.T.reshape(n // 128, 128, t).transpose(1, 0, 2).reshape(128, -1))


class Op:
    __slots__ = ("fn", "waits", "clk", "inc")


class Prog:
    ENGS = ["pe", "act", "dve", "pool", "sp"]

    def __init__(self):
        self.streams = {e: [] for e in self.ENGS}
        self.clock = {}
        self.seen = {e: {} for e in self.ENGS}
        self.lastw = {}
        self.readers = {}

    def add(self, eng, fn, reads=(), writes=(), dma_key=None):
        deps = {}

        def need(c, v):
            if deps.get(c, 0) < v:
                deps[c] = v
        for r in reads:
            w = self.lastw.get(r)
            if w is not None:
                need(*w)
        for r in writes:
            w = self.lastw.get(r)
            if w is not None:
                need(*w)
            rd = self.readers.get(r)
            if rd:
                for c, v in rd.items():
                    need(c, v)
        if dma_key is None:
            clk = eng
            inc = 1
        else:
            clk = ("dma", dma_key)
            inc = 16
        val = self.clock.get(clk, 0) + inc
        self.clock[clk] = val
        waits = []
        seen = self.seen[eng]
        for c, v in deps.items():
            if c == eng and (eng == "pe" or not SAME_SYNC):
                continue
            if seen.get(c, 0) >= v:
                continue
            seen[c] = v
            waits.append((c, v))
        op = Op()
        op.fn = fn
        op.waits = waits
        op.clk = clk
        op.inc = inc
        self.streams[eng].append(op)
        me = (clk, val)
        for r in reads:
            d = self.readers.setdefault(r, {})
            if d.get(clk, 0) < val:
                d[clk] = val
        for r in writes:
            self.lastw[r] = me
            self.readers[r] = {}
        return me

    def barrier(self, exclude=()):
        snap = {c: v for c, v in self.clock.items() if c not in exclude}
        for eng in self.ENGS:
            waits = []
            seen = self.seen[eng]
            for c, v in snap.items():
                if c == eng:
                    continue
                if seen.get(c, 0) >= v:
                    continue
                seen[c] = v
                waits.append((c, v))
            if waits:
                op = Op()
                op.fn = None
                op.waits = waits
                op.clk = None
                op.inc = 0
                self.streams[eng].append(op)
        self.lastw = {}
        self.readers = {}


class Buf:
    def __init__(self, name, ap):
        self.name = name
        self.ap = ap

    def r(self, k=0):
        return (self.name, k)

    def rs(self, n):
        return [(self.name, k) for k in range(n)]


def build_program(debug=False):
    nc = bass.Bass("TRN2", target_bir_lowering=False)
    P = Prog()

    def dram_in(name, shape):
        return nc.dram_tensor(name, list(shape), F32, kind="ExternalInput").ap()

    xT = dram_in("xT", [D, NTOK])
    cmat = dram_in("cmat", [128, 48])
    pvec = dram_in("pvec", [DEPTH, 128, NCOL])
    gfin = dram_in("gfin", [128, 16])
    w_mod = dram_in("w_mod", [DEPTH, D, 6 * D])
    w_in = dram_in("w_in", [DEPTH, D, NIN])
    w_out_a = dram_in("w_out_a", [DEPTH, D, D])
    w_out_b = dram_in("w_out_b", [DEPTH, D, D])
    w_out_c = dram_in("w_out_c", [DEPTH, D, D])
    w_o = dram_in("w_o", [DEPTH, D, D])
    w_rg_a = dram_in("w_rg_a", [DEPTH, 4096, 256])
    w_rg_x = dram_in("w_rg_x", [DEPTH, 4096, 256])
    w_up = dram_in("w_up", [DEPTH, D, 2 * DFF])
    w_down = dram_in("w_down", [DEPTH, DFF, D])
    outT = nc.dram_tensor("outT", [D, 2 * S], F32, kind="ExternalOutput").ap()
    skind = "ExternalOutput" if debug else "Internal"
    hT = nc.dram_tensor("hT", [D, NTOK], F32, kind=skind).ap()
    h2T = nc.dram_tensor("h2T", [D, NTOK], F32, kind=skind).ap()
    vT = nc.dram_tensor("vT", [D, NTOK], F32, kind=skind).ap()
    hrT = nc.dram_tensor("hrT", [D, NTOK], F32, kind=skind).ap()
    NBLK = 184
    WBd = [nc.dram_tensor(f"WB{l_}", [NBLK, 128, 4096], BF16, kind="Internal").ap() for l_ in range(DEPTH)]

    def fm(ap):
        return ap.rearrange("(k p) t -> p k t", p=128)

    xT_v, hT_v, h2T_v, vT_v, hrT_v, outT_v = fm(xT), fm(hT), fm(h2T), fm(vT), fm(hrT), fm(outT)

    es = ExitStack()
    with es:
        def sb(name, shape, dt=F32):
            return es.enter_context(nc.sbuf_tensor(name, list(shape), dt))

        PV = sb("PV", [128, DEPTH, NCOL])
        GF = sb("GF", [128, 16])
        CM = sb("CM", [128, 48])
        SC3 = sb("SC3", [128, 16, 3])
        MOD = sb("MOD", [128, DEPTH, 96, 3])
        A1 = sb("A1", [128, DEPTH, 3, 16])
        A2 = sb("A2", [128, DEPTH, 3, 16])
        CL = sb("CL", [128, DEPTH, 32])
        CL2 = sb("CL2", [128, DEPTH, 32])
        ETMP = sb("ETMP", [128, DEPTH, 32])
        ONES = sb("ONES", [128, 128])
        IDB = sb("IDB", [128, 128], BF16)
        ONESB = sb("ONESB", [128, 128], BF16)
        IDF = sb("IDF", [128, 128])
        EPSC = sb("EPSC", [128, 1])
        ZERO = sb("ZERO", [128, 1])
        WSA = sb("WSA", [128, NW, 16, 256], BF16)
        WS = [WSA[:, i] for i in range(NW)]
        BIG = sb("BIG", [128, 12288])
        XB = sb("XB", [128, 16, 512], BF16)
        XH = sb("XH", [128, 16, 128], BF16)
        MB = sb("MB", [128, 16, 512], BF16)
        TMPA = sb("TMPA", [128, NTMP * 512])
        TMPS = [TMPA[:, i * 512:(i + 1) * 512] for i in range(NTMP)]
        MU = sb("MU", [128, 512])
        RSTD = sb("RSTD", [128, 512])
        RS = sb("RS", [128, 512])
        VAR = sb("VAR", [128, 512])
        DG31 = [sb(f"DG31_{i}", [128, 31, 128], BF16) for i in range(2)]
        DG4 = [sb(f"DG4_{i}", [128, 4, 128], BF16) for i in range(2)]
        YAP = [sb(f"YAP{i}", [128, 1024], BF16) for i in range(3)]
        GSB = [sb(f"GSB{i}", [128, 640], BF16) for i in range(3)]
        psum = [es.enter_context(nc.psum_tensor(f"ps{i}", [128, 512], F32)) for i in range(8)]

        BFv = BIG[:, 0:8192].rearrange("p (k t) -> p k t", k=16)
        Zv = BIG[:, 8192:12288].bitcast(BF16).rearrange("p (k t) -> p k t", k=16)
        HIDv = BIG[:, :].bitcast(BF16).rearrange("p (k t) -> p k t", k=48)
        HHv = MB[:, :, :].rearrange("p k t -> p (k t)").bitcast(F32)[:, 0:2048].rearrange("p (k t) -> p k t", k=16)
        BF = Buf("BF", BFv)
        Z = Buf("Z", Zv)
        HID = Buf("HID", HIDv)
        HH = Buf("HH", HHv)
        X = Buf("X", XB)
        XHb = Buf("XH", XH)
        MBb = Buf("MB", MB)

        st = {"wcount": 0, "tmp": 0, "ps": 0, "w": 0, "pinned": set(), "dg31": 0, "dg4": 0, "yap": 0, "gsb": 0}

        def tmp():
            i = st["tmp"]
            st["tmp"] = (i + 1) % NTMP
            return Buf(f"TMP{i}", TMPS[i])

        def ps():
            while True:
                i = st["ps"]
                st["ps"] = (i + 1) % 8
                if i not in st["pinned"]:
                    return Buf(f"PS{i}", psum[i])

        def wslot():
            i = st["w"]
            st["w"] = (i + 1) % NW
            return Buf(f"W{i}", WS[i])

        def rot(key, arr, nm):
            i = st[key]
            st[key] = (i + 1) % len(arr)
            return Buf(f"{nm}{i}", arr[i])

        def act(out, in_, func, reads, writes, bias=None, scale=1.0):
            b = ZERO[:, 0:1] if bias is None else bias
            P.add("act", lambda e: e.activation(out=out, in_=in_, func=func, bias=b, scale=scale), reads, writes)

        def tt(eng, out, a, b, op, reads, writes):
            P.add(eng, lambda e: e.tensor_tensor(out=out, in0=a, in1=b, op=op), reads, writes)

        def ts(eng, out, a, s1, s2, op0, op1, reads, writes):
            P.add(eng, lambda e: e.tensor_scalar(out=out, in0=a, scalar1=s1, scalar2=s2, op0=op0, op1=op1), reads, writes)

        def ts1(eng, out, a, s1, op0, reads, writes):
            P.add(eng, lambda e: e.tensor_scalar(out=out, in0=a, scalar1=s1, scalar2=None, op0=op0), reads, writes)

        def stt(out, in0, scalar, in1, op0, op1, reads, writes):
            P.add("dve", lambda e: e.scalar_tensor_tensor(out=out, in0=in0, scalar=scalar, in1=in1, op0=op0, op1=op1), reads, writes)

        def cp(eng, out, in_, reads, writes):
            P.add(eng, lambda e: e.tensor_copy(out=out, in_=in_), reads, writes)

        def mm(out, pairs, reads, writes, start=True, stop=True):
            def fn(e):
                n = len(pairs)
                ins = None
                for i, (l, r) in enumerate(pairs):
                    ins = e.matmul(out, l, r, start=(start and i == 0), stop=(stop and i == n - 1))
                return ins
            P.add("pe", fn, reads, writes)

        def dma(q, out, in_, reads, writes, key):
            P.add(q, lambda e: e.dma_start(out=out, in_=in_), reads, writes, dma_key=key)

        def memset(eng, ap, val, writes):
            P.add(eng, lambda e: e.memset(ap, val), (), writes)

        def wblk_src(l, idx):
            if idx < 80:
                return w_in[l][:, idx * 256:(idx + 1) * 256]
            if idx < 112:
                m = (w_out_a, w_out_b, w_out_c, w_o)[(idx - 80) // 8]
                jb = (idx - 80) % 8
                return m[l][:, jb * 256:(jb + 1) * 256]
            if idx < 160:
                jb = idx - 112
                return w_up[l][:, jb * 256:(jb + 1) * 256]
            kb, jb = divmod(idx - 160, 8)
            return w_down[l][kb * D:(kb + 1) * D, jb * 256:(jb + 1) * 256]

        def cast_block(l, idx, key):
            dma("pool", WBd[l][idx].rearrange("p (k m) -> p k m", k=16), wblk_src(l, idx).rearrange("(k p) m -> p k m", p=128),
                [], [("WB", l, key)], key=key)

        pending_casts = []

        def wload(l, idx, grp=None):
            if pending_casts and st.get("phaseB") and st["wcount"] % 3 == 0:
                pl, pidx, pkey = pending_casts.pop(0)
                cast_block(pl, pidx, pkey)
            st["wcount"] += 1
            w = wslot()
            key = "CAST0a" if (l == 0 and 40 <= idx < 48) else ("CAST0c" if (l == 0 and idx >= 112) else f"CAST{l}b")
            dma("sp", w.ap[:, :, :], WBd[l][idx].rearrange("p (k m) -> p k m", k=16), [("WB", l, key)], [w.r()], key=w.name)
            return w

        def pvc(l, name, j):
            o = _off[name] + j
            return PV[:, l, o:o + 1]

        dma("sp", PV[:, :, :], pvec.rearrange("l p c -> p l c"), [], [("PV", 0)], key="PV")
        dma("sp", GF[:, :], gfin, [], [("GF", 0)], key="GF")
        dma("sp", CM[:, :], cmat, [], [("CM", 0)], key="CM")
        memset("dve", ONES[:, :], 1.0, [("ONES", 0)])
        memset("dve", EPSC[:, :], EPS, [("EPSC", 0)])
        memset("dve", ZERO[:, :], 0.0, [("ZERO", 0)])
        memset("dve", IDF[:, :], 1.0, [("IDF", 0)])
        P.add("pool", lambda e: e.affine_select(out=IDF[:, :], in_=IDF[:, :], pattern=[[-1, 128]], compare_op=ALU.is_equal,
                                                fill=0.0, base=0, channel_multiplier=1), [("IDF", 0)], [("IDF", 0)])
        cp("dve", IDB[:, :], IDF[:, :], [("IDF", 0)], [("IDB", 0)])
        memset("dve", ONESB[:, :], 1.0, [("ONESB", 0)])
        for a in YAP:
            memset("dve", a[:, :], 0.0, [])
        for a in GSB:
            memset("dve", a[:, :], 0.0, [])
        P.barrier()

        for idx in range(40, 48):
            cast_block(0, idx, "CAST0a")
        for idx in list(range(0, 40)) + list(range(48, 112)):
            cast_block(0, idx, "CAST0b")

        act(SC3[:, :, :].rearrange("p k n -> p (k n)"), CM[:, :], AF.Silu, [("CM", 0)], [("SC3", 0)])
        WM = [BIG[:, 0:4096].rearrange("p (k m) -> p k m", k=16), BIG[:, 4096:8192].rearrange("p (k m) -> p k m", k=16)]

        def m_finish(l, pm):
            o = _off["bmod"]
            tt("dve", MOD[:, l, :, :], pm.ap[:, 0:288].rearrange("p (j n) -> p j n", n=3),
               PV[:, l, o:o + 96].unsqueeze(2).broadcast_to([128, 96, 3]), ALU.add, [pm.r(), ("PV", 0)], [("MOD", l)])
            for r in range(3):
                stt(A1[:, l, r, :], MOD[:, l, 16:32, r], 1.0, PV[:, l, _off["g1"]:_off["g1"] + 16], ALU.add, ALU.mult,
                    [("MOD", l), ("PV", 0)], [("A1", l)])
                stt(A2[:, l, r, :], MOD[:, l, 64:80, r], 1.0, PV[:, l, _off["g2"]:_off["g2"] + 16], ALU.add, ALU.mult,
                    [("MOD", l), ("PV", 0)], [("A2", l)])
            o = _off["lam"]
            act(ETMP[:, l, :], PV[:, l, o:o + 32], AF.Exp, [("PV", 0)], [("ETMP", l)], scale=-1.0)
            ts1("dve", ETMP[:, l, :], ETMP[:, l, :], 1.0, ALU.add, [("ETMP", l)], [("ETMP", l)])
            act(ETMP[:, l, :], ETMP[:, l, :], AF.Ln, [("ETMP", l)], [("ETMP", l)])
            ts1("dve", CL[:, l, :], ETMP[:, l, :], -8.0, ALU.mult, [("ETMP", l)], [("CL", l)])
            ts1("dve", CL2[:, l, :], ETMP[:, l, :], -16.0, ALU.mult, [("ETMP", l)], [("CL2", l)])

        pm = ps()
        for jb in range(48):
            wi = jb % 2
            dma("sp", WM[wi], w_mod[0, :, jb * 256:(jb + 1) * 256].rearrange("(k p) m -> p k m", p=128),
                [], [("WM", wi)], key=f"WM{wi}")
            for mc in range(2):
                mi = jb * 2 + mc
                mm(pm.ap[:, mi * 3:mi * 3 + 3],
                   [(WM[wi][:, k, mc * 128:(mc + 1) * 128], SC3[:, k, :]) for k in range(16)],
                   [("WM", wi), ("SC3", 0)], [pm.r()])
        m_finish(0, pm)
        P.barrier(exclude={("dma", "CAST0b")})

        def modc(l, part, k, r):
            return MOD[:, l, part * 16 + k, r:r + 1]

        LAT = [(s * S + j * T, s, 8, 64, False, s, j) for s in range(2) for j in range(4)]
        CTXT = (2 * S, 2, 2, 256, True, -1, 0)

        def norm_p1(src, n, bank, k):
            sq = tmp()
            sqb = sq.ap.bitcast(BF16)
            act(sqb[:, :n], src.ap[:, k, :n], AF.Square, [src.r(k)], [sq.r()])
            mm(bank.ap[:, :n], [(ONESB[:, :], sqb[:, :n])], [sq.r(), ("ONESB", 0)], [bank.r()], start=(k == 0), stop=(k == 15))

        def norm_p2(src, n, bank, dst, dst_is_f32_store, Acol, Bcol, own_tmps=False):
            act(RS[:, :n], bank.ap[:, :n], AF.Sqrt, [bank.r()], [("RS", 0)], bias=EPSC[:, 0:1], scale=1.0 / D)
            P.add("dve", lambda e: e.reciprocal(out=RS[:, :n], in_=RS[:, :n]), [("RS", 0)], [("RS", 0)])
            for k in range(16):
                if own_tmps:
                    t1 = Buf("MU", MU) if k % 2 == 0 else Buf("VAR", VAR)
                else:
                    t1 = tmp()
                tt("dve", t1.ap[:, :n], src.ap[:, k, :n], RS[:, :n], ALU.mult, [src.r(k), ("RS", 0)], [t1.r()])
                if dst_is_f32_store is None:
                    act(dst.ap[:, k, :n], t1.ap[:, :n], AF.Identity, [t1.r()], [dst.r(k)], bias=Bcol(k), scale=Acol(k))
                else:
                    t2 = tmp()
                    act(t2.ap[:, :n], t1.ap[:, :n], AF.Identity, [t1.r()], [t2.r()], bias=Bcol(k), scale=Acol(k))
                    dst_is_f32_store(k, t2)

        def norm(src, n, dst, dst_is_f32_store, Acol, Bcol):
            bank = ps()
            for k in range(16):
                norm_p1(src, n, bank, k)
            norm_p2(src, n, bank, dst, dst_is_f32_store, Acol, Bcol)

        def proj_blocks(l, base, nblk, rhs_buf, consume, pre=None, la=3):
            nch = nblk * 2
            if pre is not None:
                for c0 in range(min(la, nch)):
                    pre(c0)
            for jb in range(nblk):
                w = wload(l, base + jb)
                for mc in range(2):
                    c = jb * 2 + mc
                    if pre is not None and c + la < nch:
                        pre(c + la)
                    pb = ps()
                    mm(pb.ap[:, :], [(w.ap[:, k, mc * 128:(mc + 1) * 128], rhs_buf.ap[:, k, :]) for k in range(16)],
                       [w.r()] + rhs_buf.rs(16), [pb.r()])
                    consume(c, pb)

        for l in range(DEPTH):
            last = (l == DEPTH - 1)
            src_v = xT_v if l == 0 else hT_v
            src_name = "xT" if l == 0 else "hT"
            Wl = w_in[l]
            tiles_all = LAT + [CTXT]
            tiles_main = LAT + ([] if last else [CTXT])

            for (tok0, mrow, nrow, R, isctx, sq_, j_) in tiles_all:
                dma("sp", BF.ap[:, :, :], src_v[:, :, tok0:tok0 + T], [(src_name, tok0)], BF.rs(16), key="BF")
                norm(BF, T, X, None, lambda k: A1[:, l, mrow, k:k + 1], lambda k: modc(l, 0, k, mrow))

                def cons_v(c, pb, tok0=tok0):
                    t1 = tmp()
                    act(t1.ap[:, :], pb.ap[:, :], AF.Identity, [pb.r()], [t1.r()], bias=pvc(l, "bin", 5 * 16 + c))
                    dma("sp" if l == 0 else "pool", vT_v[:, c, tok0:tok0 + T], t1.ap[:, :], [t1.r()], [("vT", c, tok0)], key=t1.name)
                proj_blocks(l, 40, 8, X, cons_v)
            P.barrier(exclude={("dma", "CAST0b")})

            LL = CT + S
            XBf = XB[:, :, :].rearrange("p k t -> p (k t)").bitcast(F32)
            MBf = MB[:, :, :].rearrange("p k t -> p (k t)").bitcast(F32)
            WSf = WSA[:, :, :, :].rearrange("p a k m -> p (a k m)").bitcast(F32)
            U32 = BIG[:, 0:2 * LL].rearrange("p (c t) -> p c t", c=2)
            Rs = [BIG[:, 2 * LL:3 * LL], XBf[:, 1024:1024 + LL]]
            Is = [BIG[:, 3 * LL:4 * LL], WSf[:, 0:LL]]
            As = [BIG[:, 4 * LL:5 * LL], WSf[:, LL:2 * LL]]
            Qs = [TMPA[:, 0:LL], WSf[:, 2 * LL:3 * LL]]
            H0 = TMPA[:, LL:2 * LL]
            PVb = TMPA[:, 2 * LL:2 * LL + 2320].bitcast(BF16).rearrange("p (c t) -> p c t", c=2)
            UBv = MBf[:, 0:LL].bitcast(BF16).rearrange("p (c t) -> p c t", c=2)
            WGs = [XBf[:, 0:1024].bitcast(BF16).rearrange("p (g m) -> p g m", g=8),
                   MBf[:, LL:LL + 1024].bitcast(BF16).rearrange("p (g m) -> p g m", g=8)]
            for c2 in range(2):
                memset("dve", PVb[:, c2, 0:2], 0.0, [("PVb", c2)])
                memset("dve", PVb[:, c2, 258:264], 0.0, [("PVb", c2)])
                memset("dve", PVb[:, c2, 2312:2320], 0.0, [("PVb", c2)])
            CO, LO = 0, 262
            heads = [(s_, hd_) for s_ in range(2) for hd_ in range(8)]

            def a2_loads(hi):
                s_, hd_ = heads[hi]
                WGn = WGs[hi % 2]
                for c2 in range(2):
                    c = hd_ * 2 + c2
                    dma("pool", PVb[:, c2, CO + 2:CO + 2 + CT], vT_v[:, c, 2 * S + s_ * CT: 2 * S + (s_ + 1) * CT],
                        [("vT", c, 2 * S)], [("PVb", c2)], key=f"PVbc{c2}")
                    dma("pool", PVb[:, c2, LO + 2:LO + 2 + S], vT_v[:, c, s_ * S:(s_ + 1) * S],
                        [("vT", c, s_ * S + jj * T) for jj in range(4)], [("PVb", c2)], key=f"PVbl{c2}")
                for gi, wsrc_ in enumerate((w_rg_a, w_rg_x)):
                    for d in range(2):
                        r0 = (d * 8 + hd_) * 256
                        dma("pool", WGn[:, gi * 4 + d * 2: gi * 4 + d * 2 + 2, :],
                            wsrc_[l, r0:r0 + 256, :].rearrange("(i p) m -> p i m", p=128), [], [("WG", hi % 2, gi * 2 + d)],
                            key=f"WG{hi % 2}{gi}{d}")
            a2_loads(0)
            pm1 = None
            if l == 0 and DEPTH > 1:
                pm1 = ps()
                st["pinned"].add(int(pm1.name[2:]))
                WMr = [WSf[:, 3 * LL + i * 512:3 * LL + (i + 1) * 512].rearrange("p (k m) -> p k m", k=4) for i in range(6)]
                mst = {"t": 0, "issued": 0}

                def m1_dma(t):
                    mi, kq = divmod(t, 4)
                    dma("sp", WMr[t % 6], w_mod[1, kq * 512:(kq + 1) * 512, mi * 128:(mi + 1) * 128].rearrange("(k p) m -> p k m", p=128),
                        [], [("WMr", t % 6)], key=f"WMr{t % 6}")

                def m1_tasks(n):
                    for _ in range(n):
                        t = mst["t"]
                        if t >= 384:
                            return
                        while mst["issued"] < min(384, t + 6):
                            m1_dma(mst["issued"])
                            mst["issued"] += 1
                        mi, kq = divmod(t, 4)
                        mm(pm1.ap[:, mi * 3:mi * 3 + 3], [(WMr[t % 6][:, k, :], SC3[:, kq * 4 + k, :]) for k in range(4)],
                           [("WMr", t % 6), ("SC3", 0)], [pm1.r()], start=(kq == 0), stop=(kq == 3))
                        mst["t"] = t + 1
            for hi, (s, hd) in enumerate(heads):
                WG = WGs[hi % 2]
                for c2 in range(2):
                    c = hd * 2 + c2
                    dg = rot("dg4", DG4, "DG4_")
                    ow = _off["wdwc"] + c * 4
                    tt("dve", dg.ap[:, :, :], IDF[:, :].unsqueeze(1).broadcast_to([128, 4, 128]),
                       PV[:, l, ow:ow + 4].unsqueeze(2).broadcast_to([128, 4, 128]), ALU.mult, [("IDF", 0), ("PV", 0)], [dg.r()])
                    segs = [(CO, 0, CT)] + [(LO + jj * T, CT + jj * T, T) for jj in range(4)]
                    for (po, uo, n) in segs:
                        pb = ps()
                        mm(pb.ap[:, :n], [(dg.ap[:, k, :], PVb[:, c2, po + k:po + k + n]) for k in range(4)],
                           [dg.r(), ("PVb", c2)], [pb.r()])
                        act(U32[:, c2, uo:uo + n], pb.ap[:, :n], AF.Identity, [pb.r()], [("U32", c2)], bias=pvc(l, "bdwc", c))
                    cp("pool", UBv[:, c2, :], U32[:, c2, :], [("U32", c2)], [("UB", c2)])
                if hi + 1 < len(heads):
                    a2_loads(hi + 1)
                for mo in range(2):
                    c = hd * 2 + mo
                    segs = [(0, CT)] + [(CT + jj * T, T) for jj in range(4)]
                    for d in range(2):
                        Rb, Ib, Ab, Qb = Rs[d], Is[d], As[d], Qs[d]
                        for gi, dstb, bname, rn in ((0, Rb, "brga", "R"), (1, Ib, "brgx", "I")):
                            for (uo, n) in segs:
                                pb = ps()
                                mm(pb.ap[:, :n], [(WG[:, gi * 4 + d * 2 + ic, mo * 128:(mo + 1) * 128], UBv[:, ic, uo:uo + n]) for ic in range(2)],
                                   [("WG", hi % 2, gi * 2 + d), ("UB", 0), ("UB", 1)], [pb.r()])
                                act(dstb[:, uo:uo + n], pb.ap[:, :n], AF.Sigmoid, [pb.r()], [(rn, d)],
                                    bias=pvc(l, bname, d * 16 + c))
                        if pm1 is not None:
                            m1_tasks(6)
                        tt("pool", Ib[:, :], Ib[:, :], U32[:, mo, :], ALU.mult, [("I", d), ("U32", mo)], [("I", d)])
                        act(Ab[:, :], Rb[:, :], AF.Exp, [("R", d)], [("A", d)], scale=CL[:, l, d * 16 + c: d * 16 + c + 1])
                        tt("dve", Qb[:, :], Ab[:, :], Ab[:, :], ALU.mult, [("A", d)], [("Q", d)])
                        act(Qb[:, :], Qb[:, :], AF.Sqrt, [("Q", d)], [("Q", d)], bias=ONES[:, 0:1], scale=-1.0)
                        tt("pool", Qb[:, :], Qb[:, :], Ib[:, :], ALU.mult, [("Q", d), ("I", d)], [("Q", d)])
                        if d == 0:
                            P.add("dve", lambda e, Ab=Ab, Qb=Qb: e.tensor_tensor_scan(
                                out=H0[:, 0:CT], data0=Ab[:, 0:CT], data1=Qb[:, 0:CT], initial=0.0, op0=ALU.mult, op1=ALU.add),
                                [("A", d), ("Q", d)], [("H0", 0)])
                            P.add("dve", lambda e, Ab=Ab, Qb=Qb: e.tensor_tensor_scan(
                                out=H0[:, CT:LL], data0=Ab[:, CT:LL], data1=Qb[:, CT:LL], initial=H0[:, CT - 1:CT], op0=ALU.mult, op1=ALU.add),
                                [("A", d), ("Q", d), ("H0", 0)], [("H0", 0)])
                        else:
                            P.add("dve", lambda e, Ab=Ab, Qb=Qb, Rb=Rb: e.tensor_tensor_scan(
                                out=Rb[:, 0:CT][:, ::-1], data0=Ab[:, 0:CT][:, ::-1], data1=Qb[:, 0:CT][:, ::-1], initial=0.0, op0=ALU.mult, op1=ALU.add),
                                [("A", d), ("Q", d), ("R", d)], [("R", d)])
                            P.add("dve", lambda e, Ab=Ab, Qb=Qb, Rb=Rb: e.tensor_tensor_scan(
                                out=Rb[:, CT:LL][:, ::-1], data0=Ab[:, CT:LL][:, ::-1], data1=Qb[:, CT:LL][:, ::-1], initial=Rb[:, 0:1], op0=ALU.mult, op1=ALU.add),
                                [("A", d), ("Q", d), ("R", d)], [("R", d)])
                    Io = Is[1]
                    tt("dve", Io[:, :], H0[:, :], Rs[1][:, :], ALU.add, [("H0", 0), ("R", 1), ("I", 1)], [("I", 1)])
                    dma("pool", hrT_v[:, c, s * S:(s + 1) * S], Io[:, CT:LL], [("I", 1)], [("hrT", c, s * S + jj * T) for jj in range(4)], key="HSl")
                    if not last:
                        dma("pool", hrT_v[:, c, 2 * S + s * CT:2 * S + (s + 1) * CT], Io[:, 0:CT], [("I", 1)], [("hrT", c, 2 * S)], key="HSc")
            if pm1 is not None:
                m1_tasks(400)
                st["pinned"].discard(int(pm1.name[2:]))
                m_finish(1, pm1)
            P.barrier()

            st["phaseB"] = True
            if l == 0:
                for idx in range(112, NBLK):
                    pending_casts.append((0, idx, "CAST0c"))
            if l + 1 < DEPTH:
                for idx in range(NBLK):
                    pending_casts.append((l + 1, idx, f"CAST{l + 1}b"))
            for ti_, (tok0, mrow, nrow, R, isctx, sq_, j_) in enumerate(tiles_main):
                RP = R + 30
                if ti_ == 0:
                    dma("sp", BF.ap[:, :, :], src_v[:, :, tok0:tok0 + T], [(src_name, tok0)], BF.rs(16), key="BF")
                    norm(BF, T, X, None, lambda k: A1[:, l, mrow, k:k + 1], lambda k: modc(l, 0, k, mrow))
                s1 = ps()
                st["pinned"].add(int(s1.name[2:]))
                s2 = ps()
                st["pinned"].add(int(s2.name[2:]))
                wva, wga = {}, {}

                def a_stage1(c):
                    jb, mc = divmod(c, 2)
                    if mc == 0:
                        wva[jb] = wload(l, 0 * 8 + jb)
                        wga[jb] = wload(l, 1 * 8 + jb)
                    wv_, wg_ = wva[jb], wga[jb]
                    pa = ps()
                    mm(pa.ap[:, :], [(wv_.ap[:, k, mc * 128:(mc + 1) * 128], X.ap[:, k, :]) for k in range(16)], [wv_.r()] + X.rs(16), [pa.r()])
                    pg = ps()
                    mm(pg.ap[:, :], [(wg_.ap[:, k, mc * 128:(mc + 1) * 128], X.ap[:, k, :]) for k in range(16)], [wg_.r()] + X.rs(16), [pg.r()])
                    sg = tmp()
                    act(sg.ap[:, :], pg.ap[:, :], AF.Sigmoid, [pg.r()], [sg.r()], bias=pvc(l, "bin", 16 + c))
                    ya = rot("yap", YAP, "YAP")
                    yav = ya.ap[:, 0:nrow * RP].rearrange("p (r c) -> p r c", r=nrow)
                    stt(yav[:, :, 15:15 + R], pa.ap[:, :].rearrange("p (r c) -> p r c", r=nrow), pvc(l, "bin", c),
                        sg.ap[:, :].rearrange("p (r c) -> p r c", r=nrow), ALU.add, ALU.mult, [pa.r(), sg.r()], [ya.r()])
                    dg = rot("dg31", DG31, "DG31_")
                    ow = _off["wdwa"] + c * 31
                    tt("dve", dg.ap[:, :, :], IDF[:, :].unsqueeze(1).broadcast_to([128, 31, 128]),
                       PV[:, l, ow:ow + 31].unsqueeze(2).broadcast_to([128, 31, 128]), ALU.mult, [("IDF", 0), ("PV", 0)], [dg.r()])
                    return (ya, yav, dg)

                def a_stage2(c, stt_):
                    ya, yav, dg = stt_
                    pc_ = ps()
                    mm(pc_.ap[:, :].rearrange("p (r c) -> p r c", r=nrow), [(dg.ap[:, k, :], yav[:, :, k:k + R]) for k in range(31)],
                       [dg.r(), ya.r()], [pc_.r()])
                    act(BF.ap[:, c, :], pc_.ap[:, :], AF.Identity, [pc_.r()], [BF.r(c)])
                    sq = tmp()
                    sqb = sq.ap.bitcast(BF16)
                    tt("dve", sqb[:, 0:T], BF.ap[:, c, :], BF.ap[:, c, :], ALU.mult, [BF.r(c)], [sq.r()])
                    mm(s1.ap[:, :], [(ONES[:, :], BF.ap[:, c, :])], [BF.r(c)], [s1.r()], start=(c == 0), stop=(c == 15))
                    mm(s2.ap[:, :], [(ONESB[:, :], sqb[:, 0:T])], [sq.r()], [s2.r()], start=(c == 0), stop=(c == 15))
                cur = a_stage1(0)
                for c in range(16):
                    nxt_s = a_stage1(c + 1) if c + 1 < 16 else None
                    a_stage2(c, cur)
                    cur = nxt_s
                ts1("dve", MU[:, :], s1.ap[:, :], 1.0 / D, ALU.mult, [s1.r()], [("MU", 0)])
                tt("dve", VAR[:, :], MU[:, :], MU[:, :], ALU.mult, [("MU", 0)], [("VAR", 0)])
                stt(VAR[:, :], s2.ap[:, :], 1.0 / D, VAR[:, :], ALU.mult, ALU.subtract, [s2.r(), ("VAR", 0)], [("VAR", 0)])
                act(RSTD[:, :], VAR[:, :], AF.Sqrt, [("VAR", 0)], [("RSTD", 0)], bias=EPSC[:, 0:1])
                P.add("dve", lambda e: e.reciprocal(out=RSTD[:, :], in_=RSTD[:, :]), [("RSTD", 0)], [("RSTD", 0)])
                st["pinned"].discard(int(s1.name[2:]))
                st["pinned"].discard(int(s2.name[2:]))

                def ga_half(h):
                    for jb in range(4 * h, 4 * h + 4):
                        w2 = wload(l, 7 * 8 + jb)
                        for mc in range(2):
                            c = jb * 2 + mc
                            p2 = ps()
                            mm(p2.ap[:, :], [(w2.ap[:, k, mc * 128:(mc + 1) * 128], X.ap[:, k, :]) for k in range(16)], [w2.r()] + X.rs(16), [p2.r()])
                            act(MBb.ap[:, c, :], p2.ap[:, :], AF.Sigmoid, [p2.r()], [MBb.r(c)], bias=pvc(l, "bin", 7 * 16 + c))
                ga_half(0)
                for c in range(16):
                    t1 = tmp()
                    tt("dve", t1.ap[:, :], BF.ap[:, c, :], MU[:, :], ALU.subtract, [BF.r(c), ("MU", 0)], [t1.r()])
                    tt("dve", t1.ap[:, :], t1.ap[:, :], RSTD[:, :], ALU.mult, [t1.r(), ("RSTD", 0)], [t1.r()])
                    act(Z.ap[:, c, :], t1.ap[:, :], AF.Silu, [t1.r()], [Z.r(c)], bias=pvc(l, "lnb", c), scale=pvc(l, "lng", c))
                ga_half(1)
                for jb in range(8):
                    w1 = wload(l, 80 + jb)
                    for mc in range(2):
                        c = jb * 2 + mc
                        p1 = ps()
                        mm(p1.ap[:, :], [(w1.ap[:, k, mc * 128:(mc + 1) * 128], Z.ap[:, k, :]) for k in range(16)], [w1.r()] + Z.rs(16), [p1.r()])
                        tt("dve", BF.ap[:, c, :], p1.ap[:, :], MBb.ap[:, c, :], ALU.mult, [p1.r(), MBb.r(c)], [BF.r(c)])


                def outproj(wout, gseg, first, lastb):
                    for jb in range(8):
                        w1 = wload(l, wout + jb)
                        w2 = wload(l, gseg * 8 + jb)
                        for mc in range(2):
                            c = jb * 2 + mc
                            p1 = ps()
                            mm(p1.ap[:, :], [(w1.ap[:, k, mc * 128:(mc + 1) * 128], Z.ap[:, k, :]) for k in range(16)], [w1.r()] + Z.rs(16), [p1.r()])
                            p2 = ps()
                            mm(p2.ap[:, :], [(w2.ap[:, k, mc * 128:(mc + 1) * 128], X.ap[:, k, :]) for k in range(16)], [w2.r()] + X.rs(16), [p2.r()])
                            sg = tmp()
                            act(sg.ap[:, :], p2.ap[:, :], AF.Sigmoid, [p2.r()], [sg.r()], bias=pvc(l, "bin", gseg * 16 + c))
                            if first:
                                tt("dve", BF.ap[:, c, :], p1.ap[:, :], sg.ap[:, :], ALU.mult, [p1.r(), sg.r()], [BF.r(c)])
                            else:
                                tt("dve", sg.ap[:, :], p1.ap[:, :], sg.ap[:, :], ALU.mult, [p1.r(), sg.r()], [sg.r()])
                                if lastb:
                                    tt("dve", MBb.ap[:, c, :], BF.ap[:, c, :], sg.ap[:, :], ALU.add, [BF.r(c), sg.r()], [MBb.r(c)])
                                else:
                                    tt("dve", BF.ap[:, c, :], BF.ap[:, c, :], sg.ap[:, :], ALU.add, [BF.r(c), sg.r()], [BF.r(c)])
                R2 = R + 2
                wcs, whs, wbs = {}, {}, {}

                def b_stage1(c):
                    jb, mc = divmod(c, 2)
                    if mc == 0:
                        wcs[jb] = wload(l, 3 * 8 + jb)
                        whs[jb] = wload(l, 4 * 8 + jb)
                        wbs[jb] = wload(l, 2 * 8 + jb)
                    wc_, wh_, wb_ = wcs[jb], whs[jb], wbs[jb]
                    pcc = ps()
                    mm(pcc.ap[:, :], [(wc_.ap[:, k, mc * 128:(mc + 1) * 128], X.ap[:, k, :]) for k in range(16)], [wc_.r()] + X.rs(16), [pcc.r()])
                    ph = ps()
                    mm(ph.ap[:, :], [(wh_.ap[:, k, mc * 128:(mc + 1) * 128], X.ap[:, k, :]) for k in range(16)], [wh_.r()] + X.rs(16), [ph.r()])
                    pbb = ps()
                    mm(pbb.ap[:, :], [(wb_.ap[:, k, mc * 128:(mc + 1) * 128], X.ap[:, k, :]) for k in range(16)], [wb_.r()] + X.rs(16), [pbb.r()])
                    scb = tmp()
                    act(scb.ap[:, :], pcc.ap[:, :], AF.Identity, [pcc.r()], [scb.r()], bias=pvc(l, "bin", 3 * 16 + c))
                    ya = rot("yap", YAP, "YAP")
                    yav = ya.ap[:, 0:nrow * R2].rearrange("p (r c) -> p r c", r=nrow)
                    memset("dve", yav[:, :, 0:1], 0.0, [ya.r()])
                    memset("dve", yav[:, :, R + 1:R + 2], 0.0, [ya.r()])
                    stt(yav[:, :, 1:1 + R], ph.ap[:, :].rearrange("p (r c) -> p r c", r=nrow), pvc(l, "bin", 4 * 16 + c),
                        scb.ap[:, :].rearrange("p (r c) -> p r c", r=nrow), ALU.add, ALU.mult, [ph.r(), scb.r()], [ya.r()])
                    dg = rot("dg4", DG4, "DG4_")
                    ow = _off["wdwb"] + c * 3
                    tt("dve", dg.ap[:, 0:3, :], IDF[:, :].unsqueeze(1).broadcast_to([128, 3, 128]),
                       PV[:, l, ow:ow + 3].unsqueeze(2).broadcast_to([128, 3, 128]), ALU.mult, [("IDF", 0), ("PV", 0)], [dg.r()])
                    sbb = tmp()
                    act(sbb.ap[:, :], pbb.ap[:, :], AF.Identity, [pbb.r()], [sbb.r()], bias=pvc(l, "bin", 2 * 16 + c))
                    return (ya, yav, dg, sbb)

                def b_stage2(c, stt_):
                    ya, yav, dg, sbb = stt_
                    pcv = ps()
                    mm(pcv.ap[:, :].rearrange("p (r c) -> p r c", r=nrow), [(dg.ap[:, k, :], yav[:, :, k:k + R]) for k in range(3)],
                       [dg.r(), ya.r()], [pcv.r()])
                    tt("dve", Z.ap[:, c, :], pcv.ap[:, :], sbb.ap[:, :], ALU.mult, [pcv.r(), sbb.r()], [Z.r(c)])
                cur = b_stage1(0)
                for c in range(16):
                    nxt_s = b_stage1(c + 1) if c + 1 < 16 else None
                    b_stage2(c, cur)
                    cur = nxt_s
                for a_i, a in enumerate(YAP):
                    memset("dve", a[:, :], 0.0, [(f"YAP{a_i}", 0)])
                outproj(88, 8, False, False)

                hrs = {}

                def pre_c(c, tok0=tok0, hrs=hrs):
                    hr = tmp()
                    dma("sp", hr.ap[:, :], hrT_v[:, c, tok0:tok0 + T], [("hrT", c, tok0)], [hr.r()], key=hr.name)
                    hrs[c] = hr

                def cons_c(c, pb, tok0=tok0, hrs=hrs):
                    gl = tmp()
                    act(gl.ap[:, :], pb.ap[:, :], AF.Gelu_apprx_tanh, [pb.r()], [gl.r()], bias=pvc(l, "bin", 6 * 16 + c))
                    hr = hrs.pop(c)
                    tt("dve", Z.ap[:, c, :], gl.ap[:, :], hr.ap[:, :], ALU.mult, [gl.r(), hr.r()], [Z.r(c)])
                proj_blocks(l, 48, 8, X, cons_c, pre=pre_c)
                outproj(96, 9, False, True)

                hcs = {}
                nxt = tiles_main[ti_ + 1] if ti_ + 1 < len(tiles_main) else None
                nbank = None
                if nxt is not None:
                    ntok0, nmrow = nxt[0], nxt[1]
                    dma("sp", BF.ap[:, :, :], src_v[:, :, ntok0:ntok0 + T], [(src_name, ntok0)], BF.rs(16), key="BF")
                    nbank = ps()
                    st["pinned"].add(int(nbank.name[2:]))

                def pre_o(c, tok0=tok0, hcs=hcs):
                    hc = tmp()
                    dma("sp", hc.ap[:, :], src_v[:, c, tok0:tok0 + T], [(src_name, tok0)], [hc.r()], key=hc.name)
                    hcs[c] = hc

                def cons_o(c, pb, tok0=tok0, mrow=mrow, hcs=hcs, nbank=nbank):
                    hc = hcs.pop(c)
                    stt(hc.ap[:, :], pb.ap[:, :], modc(l, 2, c, mrow), hc.ap[:, :], ALU.mult, ALU.add, [pb.r(), hc.r()], [hc.r()])
                    dma("pool", h2T_v[:, c, tok0:tok0 + T], hc.ap[:, :], [hc.r()], [("h2T", tok0)], key=hc.name)
                    if nbank is not None:
                        if c < 8:
                            norm_p1(BF, T, nbank, 2 * c)
                            norm_p1(BF, T, nbank, 2 * c + 1)
                        if c == 8:
                            st["pinned"].discard(int(nbank.name[2:]))
                            norm_p2(BF, T, nbank, X, None, lambda k, nmrow=nmrow: A1[:, l, nmrow, k:k + 1], lambda k, nmrow=nmrow: modc(l, 0, k, nmrow), own_tmps=True)
                proj_blocks(l, 104, 8, MBb, cons_o, pre=pre_o)
            P.barrier()

            for (tok0, mrow, nrow, R, isctx, sq_, j_) in tiles_main:
                has_prev = (not isctx) and j_ > 0
                has_next = (not isctx) and j_ < 3
                dma("sp", BF.ap[:, :, :], h2T_v[:, :, tok0:tok0 + T], [("h2T", tok0)], BF.rs(16) + HID.rs(48), key="BF")
                Acol = lambda k: A2[:, l, mrow, k:k + 1]
                Bcol = lambda k: modc(l, 3, k, mrow)
                norm(BF, T, X, None, Acol, Bcol)
                if has_prev:
                    dma("sp", HH.ap[:, :, 0:64], h2T_v[:, :, tok0 - 64:tok0], [("h2T", tok0 - T)], [HH.r(0)], key="HH0")
                if has_next:
                    dma("sp", HH.ap[:, :, 64:128], h2T_v[:, :, tok0 + T:tok0 + T + 64], [("h2T", tok0 + T)], [HH.r(0)], key="HH1")
                h0 = 0 if has_prev else 64
                h1 = 128 if has_next else 64
                nh = h1 - h0
                if nh > 0:
                    HHs = Buf("HH", HH.ap[:, :, h0:h1])
                    XHs = Buf("XH", XHb.ap[:, :, h0:h1])
                    HHs.r = lambda k=0: ("HH", 0)
                    XHs.r = lambda k=0: ("XH", 0)
                    norm(HHs, nh, XHs, None, Acol, Bcol)
                wgs, wvs = {}, {}

                def b2_stage1(c):
                    jb, mc = divmod(c, 2)
                    if mc == 0:
                        wgs[jb] = wload(l, 112 + jb)
                        wvs[jb] = wload(l, 136 + jb)
                    wg_ = wgs[jb]
                    pa = ps()
                    mm(pa.ap[:, :], [(wg_.ap[:, k, mc * 128:(mc + 1) * 128], X.ap[:, k, :]) for k in range(16)], [wg_.r()] + X.rs(16), [pa.r()])
                    g = rot("gsb", GSB, "GSB")
                    ow = _off["wdwf"] + c * 3
                    dg = rot("dg4", DG4, "DG4_")
                    tt("dve", dg.ap[:, 0:3, :], IDF[:, :].unsqueeze(1).broadcast_to([128, 3, 128]),
                       PV[:, l, ow:ow + 3].unsqueeze(2).broadcast_to([128, 3, 128]), ALU.mult, [("IDF", 0), ("PV", 0)], [dg.r()])
                    if isctx:
                        gv = g.ap[:, 0:2 * 258].rearrange("p (r c) -> p r c", r=2)
                        memset("dve", gv[:, :, 0:1], 0.0, [g.r()])
                        memset("dve", gv[:, :, 257:258], 0.0, [g.r()])
                        act(gv[:, :, 1:257], pa.ap[:, :].rearrange("p (r c) -> p r c", r=2), AF.Identity, [pa.r()], [g.r()])
                    else:
                        if nh > 0:
                            phh = ps()
                            mm(phh.ap[:, 0:nh], [(wg_.ap[:, k, mc * 128:(mc + 1) * 128], XHb.ap[:, k, h0:h1]) for k in range(16)],
                               [wg_.r(), ("XH", 0)], [phh.r()])
                        if has_prev:
                            cp("dve", g.ap[:, 0:64], phh.ap[:, 0:64], [phh.r()], [g.r()])
                        else:
                            memset("dve", g.ap[:, 0:64], 0.0, [g.r()])
                        if has_next:
                            cp("dve", g.ap[:, 576:640], phh.ap[:, 64 - h0:128 - h0], [phh.r()], [g.r()])
                        else:
                            memset("dve", g.ap[:, 576:640], 0.0, [g.r()])
                        act(g.ap[:, 64:576], pa.ap[:, :], AF.Identity, [pa.r()], [g.r()])
                    return (g, dg)

                def b2_stage2(c, stt_):
                    g, dg = stt_
                    jb, mc = divmod(c, 2)
                    wv_ = wvs[jb]
                    pcv = ps()
                    if isctx:
                        gv = g.ap[:, 0:2 * 258].rearrange("p (r c) -> p r c", r=2)
                        mm(pcv.ap[:, :].rearrange("p (r c) -> p r c", r=2), [(dg.ap[:, k, :], gv[:, :, k:k + 256]) for k in range(3)],
                           [dg.r(), g.r()], [pcv.r()])
                    else:
                        mm(pcv.ap[:, :], [(dg.ap[:, k, :], g.ap[:, 64 * k:64 * k + 512]) for k in range(3)], [dg.r(), g.r()], [pcv.r()])
                    gg = tmp()
                    act(gg.ap[:, :], pcv.ap[:, :], AF.Gelu_apprx_tanh, [pcv.r()], [gg.r()])
                    pv_ = ps()
                    mm(pv_.ap[:, :], [(wv_.ap[:, k, mc * 128:(mc + 1) * 128], X.ap[:, k, :]) for k in range(16)], [wv_.r()] + X.rs(16), [pv_.r()])
                    tt("dve", HID.ap[:, c, :], pv_.ap[:, :], gg.ap[:, :], ALU.mult, [pv_.r(), gg.r()], [HID.r(c)] + ([BF.r(c // 2)] if c < 32 else []))
                cur = b2_stage1(0)
                for c in range(48):
                    nxt_s = b2_stage1(c + 1) if c + 1 < 48 else None
                    b2_stage2(c, cur)
                    cur = nxt_s
                hcs = {}

                def pre_d(c, tok0=tok0, hcs=hcs):
                    hc = tmp()
                    dma("sp", hc.ap[:, :], h2T_v[:, c, tok0:tok0 + T], [("h2T", tok0)], [hc.r()], key=hc.name)
                    hcs[c] = hc
                for c0 in range(3):
                    pre_d(c0)
                for jb in range(8):
                    wk = [wload(l, 160 + kb * 8 + jb) for kb in range(3)]
                    for mc in range(2):
                        c = jb * 2 + mc
                        if c + 3 < 16:
                            pre_d(c + 3)
                        pb = ps()
                        pairs = []
                        for kb in range(3):
                            pairs += [(wk[kb].ap[:, k, mc * 128:(mc + 1) * 128], HID.ap[:, kb * 16 + k, :]) for k in range(16)]
                        mm(pb.ap[:, :], pairs, [w_.r() for w_ in wk] + HID.rs(48), [pb.r()])
                        hc = hcs.pop(c)
                        stt(hc.ap[:, :], pb.ap[:, :], modc(l, 5, c, mrow), hc.ap[:, :], ALU.mult, ALU.add, [pb.r(), hc.r()], [hc.r()])
                        dma("pool", hT_v[:, c, tok0:tok0 + T], hc.ap[:, :], [hc.r()], [("hT", tok0)], key=hc.name)
            P.barrier()

        for (tok0, mrow, nrow, R, isctx, sq_, j_) in LAT:
            dma("sp", BF.ap[:, :, :], hT_v[:, :, tok0:tok0 + T], [("hT", tok0)], BF.rs(16), key="BF")

            def store(k, t2, tok0=tok0):
                dma("pool", outT_v[:, k, tok0:tok0 + T], t2.ap[:, :], [t2.r()], [("outT", k, tok0)], key=t2.name)
            norm(BF, T, None, store, lambda k: GF[:, k:k + 1], lambda k: ZERO[:, 0:1])
        P.barrier()

        sems = {}
        for clk in P.clock:
            nm = "s_" + "".join(ch for ch in str(clk) if ch.isalnum())
            sems[clk] = es.enter_context(nc.semaphore(nm))
        block = es.enter_context(nc.Block())
        bname = {"pe": "tensor", "act": "scalar", "dve": "vector", "pool": "gpsimd", "sp": "sync"}

        def make(engname):
            def body(e):
                for op in P.streams[engname]:
                    for (c, v) in op.waits:
                        e.wait_ge(sems[c], v)
                    if op.fn is not None:
                        ins = op.fn(e)
                        ins.then_inc(sems[op.clk], op.inc)
            return body
        for engname in Prog.ENGS:
            getattr(block, bname[engname])(make(engname))
    return nc


_NC_CACHE = {}


def _prep_inputs(inp):
    f = lambda a: np.ascontiguousarray(np.asarray(a, dtype=np.float32))
    x, c, ctx, c_ctx = f(inp["x"]), f(inp["c"]), f(inp["ctx"]), f(inp["c_ctx"])
    pv = np.zeros((DEPTH, 128, NCOL), np.float32)
    for l in range(DEPTH):
        def put(name, arr):
            pv[l, :, _off[name]:_off[name] + arr.shape[1]] = arr
        put("bmod", _pk(f(inp["b_mod"])[l]))
        put("g1", _pk(f(inp["g_norm1"])[l]))
        put("g2", _pk(f(inp["g_norm2"])[l]))
        put("bin", _pk(f(inp["b_in"])[l]))
        put("wdwa", _pkt(f(inp["w_dw_a"])[l]))
        put("lng", _pk(f(inp["ln_a_g"])[l]))
        put("lnb", _pk(f(inp["ln_a_b"])[l]))
        put("wdwb", _pkt(f(inp["w_dw_b"])[l]))
        put("wdwc", _pkt(f(inp["w_dw_c"])[l]))
        put("bdwc", _pk(f(inp["b_dw_c"])[l]))
        put("brga", _pk(f(inp["b_rg_a"])[l].reshape(-1)))
        put("brgx", _pk(f(inp["b_rg_x"])[l].reshape(-1)))
        put("lam", _pk(f(inp["lam"])[l].reshape(-1)))
        put("wdwf", _pkt(f(inp["w_dw_f"])[l]))
    shared = {
        "pvec": pv, "gfin": _pk(f(inp["g_final"])),
        "w_mod": f(inp["w_mod"]), "w_in": f(inp["w_in"]), "w_out_a": f(inp["w_out_a"]), "w_out_b": f(inp["w_out_b"]),
        "w_out_c": f(inp["w_out_c"]), "w_o": f(inp["w_o"]),
        "w_rg_a": f(inp["w_rg_a"]).reshape(DEPTH, 4096, 256), "w_rg_x": f(inp["w_rg_x"]).reshape(DEPTH, 4096, 256),
        "w_up": f(inp["w_up"]), "w_down": f(inp["w_down"]),
    }
    in_maps = []
    for i in range(NCORES):
        b0, b1 = 2 * i, 2 * i + 1
        xT = np.ascontiguousarray(np.concatenate([x[b0].T, x[b1].T, ctx[b0].T, ctx[b1].T], axis=1))
        cm = np.stack([c[b0], c[b1], c_ctx], axis=1)
        cm = np.ascontiguousarray(cm.reshape(16, 128, 3).transpose(1, 0, 2).reshape(128, 48))
        m = dict(shared)
        m["xT"] = xT
        m["cmat"] = cm
        in_maps.append(m)
    return in_maps


def kernel(**inputs):
    in_maps = _prep_inputs(inputs)
    if "nc" not in _NC_CACHE:
        _NC_CACHE["nc"] = build_program()
    nc = _NC_CACHE["nc"]
    res = run_bass_kernel_spmd(nc, in_maps, core_ids=list(range(NCORES)))
    out = np.empty((2 * NCORES, S, D), np.float32)
    for i in range(NCORES):
        oT = res.results[i]["outT"]
        out[2 * i] = oT[:, 0:S].T
        out[2 * i + 1] = oT[:, S:2 * S].T
    return out
```
